# Optimizing a Trainium2 kernel written in Bass

```python
import math
import jax, jax.numpy as jnp
from jax import lax
import numpy as np

D_MODEL = 2048
BATCH = 4
SEQ = 2048
DEPTH = 1
DEC_BATCH = 128
DEC_SEQ = 8
PAST_LEN = 16384
PAGE_SIZE = 128

CHUNK = 128
A_GROUPS = 4
A_GROUP_DIM = 256
A_WIDTH = A_GROUPS * A_GROUP_DIM
R_HEADS = 8
R_QK_DIM = 128
R_V_DIM = 256
R_QK_WIDTH = R_HEADS * R_QK_DIM
R_V_WIDTH = R_HEADS * R_V_DIM
R_CHUNK = 128
N_MEM = 256
M_HEADS = 4
M_HEAD_DIM = 256
M_WIDTH = M_HEADS * M_HEAD_DIM
N_BRANCH = 3
D_FF = -(-8 * D_MODEL // (3 * 256)) * 256
ALPHA = (2.0 * DEPTH) ** 0.25
BETA = (8.0 * DEPTH) ** -0.25
ROPE_BASE = 10000.0
LN_EPS = 1e-5

OFF_AU = N_BRANCH * D_MODEL
OFF_AV = OFF_AU + A_WIDTH
OFF_RQ = OFF_AV + A_WIDTH
OFF_RK = OFF_RQ + R_QK_WIDTH
OFF_RV = OFF_RK + R_QK_WIDTH
OFF_RG = OFF_RV + R_V_WIDTH
OFF_MQ = OFF_RG + R_V_WIDTH
IN_WIDTH = OFF_MQ + M_WIDTH

kernel_name = "gated_hybrid_sgu_retention_memory_step"


def _standardize(x, eps=LN_EPS):
    xf = x.astype(jnp.float32)
    mu = jnp.mean(xf, axis=-1, keepdims=True)
    xc = xf - mu
    var = jnp.mean(xc * xc, axis=-1, keepdims=True)
    return xc * lax.rsqrt(var + eps)


def _rotary(x, pos):
    half = x.shape[-1] // 2
    inv = ROPE_BASE ** (-jnp.arange(half, dtype=jnp.float32) / half)
    ang = pos[:, None] * inv[None, :]
    cos = jnp.cos(ang)[None, :, None, :]
    sin = jnp.sin(ang)[None, :, None, :]
    xf = x.astype(jnp.float32)
    x1, x2 = xf[..., :half], xf[..., half:]
    return jnp.concatenate([x1 * cos - x2 * sin, x1 * sin + x2 * cos], axis=-1)


def _retention(q, k, v, s0):
    B, T, H, dk = q.shape
    dv = v.shape[-1]
    C = min(T, R_CHUNK)
    n = T // C
    log_g = jnp.log1p(-jnp.exp2(-5.0 - jnp.arange(H, dtype=jnp.float32)))
    idx = jnp.arange(C, dtype=jnp.float32)
    diff = idx[:, None] - idx[None, :]
    causal = diff >= 0
    dmask = jnp.where(causal[None], jnp.exp(jnp.where(causal, diff, 0.0)[None] * log_g[:, None, None]), 0.0)
    q_decay = jnp.exp((idx[:, None] + 1.0) * log_g[None, :])
    k_decay = jnp.exp((C - 1.0 - idx)[:, None] * log_g[None, :])
    chunk_decay = jnp.exp(C * log_g)

    def to_chunks(a):
        return a.astype(jnp.float32).reshape(B, n, C, H, a.shape[-1]).transpose(1, 0, 2, 3, 4)

    def step(s, inp):
        qc, kc, vc = inp
        scores = jnp.einsum('bihd,bjhd->bhij', qc, kc) * dmask[None]
        o = (jnp.einsum('bhij,bjhe->bihe', scores, vc)
             + jnp.einsum('bihd,bhde->bihe', qc, s) * q_decay[None, :, :, None])
        s_new = (s * chunk_decay[None, :, None, None]
                 + jnp.einsum('bjhd,bjhe->bhde', kc * k_decay[None, :, :, None], vc))
        return s_new, o

    s_fin, o = lax.scan(step, s0.astype(jnp.float32), (to_chunks(q), to_chunks(k), to_chunks(v)))
    o = o.transpose(1, 0, 2, 3, 4).reshape(B, T, H, dv)
    return o, s_fin


def _spatial_gate(u, v, w_s, b_s):
    B, T, G, dg = v.shape
    C = min(T, CHUNK)
    n = T // C
    w = jnp.tril(w_s[:, :C, :C])
    vc = v.reshape(B, n, C, G, dg)
    z = jnp.einsum('gij,bnjgd->bnigd', w, vc) + b_s[:, :C].T[None, None, :, :, None]
    return u * z.reshape(B, T, G, dg)


def _mem_attend(q, mk, mv):
    s = jnp.einsum('bthd,bmhd->bhtm', q.astype(jnp.float32), mk.astype(jnp.float32)) * (M_HEAD_DIM ** -0.5)
    p = jax.nn.softmax(s, axis=-1)
    return jnp.einsum('bhtm,bmhe->bthe', p, mv.astype(jnp.float32))


def _layer(x, pos_start, ret_s0, mem_k, mem_v, p):
    B, T, _ = x.shape
    pos = pos_start + jnp.arange(T, dtype=jnp.float32)
    h = x @ p['w_in']
    gates = jax.nn.sigmoid(h[..., :OFF_AU].astype(jnp.float32)).reshape(B, T, N_BRANCH, D_MODEL)
    u = jax.nn.gelu(h[..., OFF_AU:OFF_AV]).reshape(B, T, A_GROUPS, A_GROUP_DIM)
    va = jax.nn.gelu(h[..., OFF_AV:OFF_RQ]).reshape(B, T, A_GROUPS, A_GROUP_DIM)
    va = _standardize(va) * p['sgu_ln_g'] + p['sgu_ln_b']
    ya = _spatial_gate(u, va, p['sgu_w'], p['sgu_b']).reshape(B, T, A_WIDTH) @ p['w_proj_a']
    q = _rotary(h[..., OFF_RQ:OFF_RK].reshape(B, T, R_HEADS, R_QK_DIM), pos)
    k = _rotary(h[..., OFF_RK:OFF_RV].reshape(B, T, R_HEADS, R_QK_DIM), pos) * (R_QK_DIM ** -0.5)
    vr = h[..., OFF_RV:OFF_RG].reshape(B, T, R_HEADS, R_V_DIM)
    o, s_fin = _retention(q, k, vr, ret_s0)
    o = (_standardize(o) * p['ret_gn_g'].reshape(R_HEADS, R_V_DIM)).reshape(B, T, R_V_WIDTH)
    yb = (jax.nn.silu(h[..., OFF_RG:OFF_MQ].astype(jnp.float32)) * o) @ p['w_proj_b']
    qm = h[..., OFF_MQ:].reshape(B, T, M_HEADS, M_HEAD_DIM)
    yc = _mem_attend(qm, mem_k, mem_v).reshape(B, T, M_WIDTH) @ p['w_proj_c']
    merged = gates[:, :, 0] * ya + gates[:, :, 1] * yb + gates[:, :, 2] * yc
    x1 = _standardize(ALPHA * x + merged @ p['w_out']) * p['ln1_g'] + p['ln1_b']
    f = (jax.nn.silu(x1 @ p['w_ffn_gate']) * (x1 @ p['w_ffn_up'])) @ p['w_ffn_down']
    y = _standardize(ALPHA * x1 + f) * p['ln2_g'] + p['ln2_b']
    return y, s_fin, va


def setup_inputs(seed: int = 0) -> dict:
    key = jax.random.key(seed)
    ks = jax.random.split(key, 32)
    f32 = jnp.float32

    def nrm(k, shape, scale):
        return jax.random.normal(k, shape, f32) * scale

    L = DEPTH
    return {
        'x_prompt': nrm(ks[0], (BATCH, SEQ, D_MODEL), 1.0),
        'x_sample': nrm(ks[1], (DEC_BATCH, DEC_SEQ, D_MODEL), 1.0),
        'mem_prompt': nrm(ks[2], (BATCH, N_MEM, D_MODEL), 1.0),
        'state_ret': nrm(ks[3], (L, DEC_BATCH, R_HEADS, R_QK_DIM, R_V_DIM), 0.1),
        'cache_mem_k': nrm(ks[4], (L, DEC_BATCH, N_MEM, M_HEADS, M_HEAD_DIM), 1.0),
        'cache_mem_v': nrm(ks[5], (L, DEC_BATCH, N_MEM, M_HEADS, M_HEAD_DIM), 1.0),
        'w_in': nrm(ks[6], (L, D_MODEL, IN_WIDTH), D_MODEL ** -0.5),
        'sgu_ln_g': 1.0 + nrm(ks[7], (L, A_GROUPS, A_GROUP_DIM), 0.02),
        'sgu_ln_b': nrm(ks[8], (L, A_GROUPS, A_GROUP_DIM), 0.02),
        'sgu_w': nrm(ks[9], (L, A_GROUPS, CHUNK, CHUNK), CHUNK ** -0.5),
        'sgu_b': 1.0 + nrm(ks[10], (L, A_GROUPS, CHUNK), 0.02),
        'w_proj_a': nrm(ks[11], (L, A_WIDTH, D_MODEL), BETA * A_WIDTH ** -0.5),
        'ret_gn_g': 1.0 + nrm(ks[12], (L, R_V_WIDTH), 0.02),
        'w_proj_b': nrm(ks[13], (L, R_V_WIDTH, D_MODEL), BETA * R_V_WIDTH ** -0.5),
        'w_mem_k': nrm(ks[14], (L, D_MODEL, M_WIDTH), D_MODEL ** -0.5),
        'w_mem_v': nrm(ks[15], (L, D_MODEL, M_WIDTH), D_MODEL ** -0.5),
        'w_proj_c': nrm(ks[16], (L, M_WIDTH, D_MODEL), BETA * M_WIDTH ** -0.5),
        'w_out': nrm(ks[17], (L, D_MODEL, D_MODEL), BETA * D_MODEL ** -0.5),
        'ln1_g': 1.0 + nrm(ks[18], (L, D_MODEL), 0.02),
        'ln1_b': nrm(ks[19], (L, D_MODEL), 0.02),
        'w_ffn_gate': nrm(ks[20], (L, D_MODEL, D_FF), D_MODEL ** -0.5),
        'w_ffn_up': nrm(ks[21], (L, D_MODEL, D_FF), D_MODEL ** -0.5),
        'w_ffn_down': nrm(ks[22], (L, D_FF, D_MODEL), BETA * D_FF ** -0.5),
        'ln2_g': 1.0 + nrm(ks[23], (L, D_MODEL), 0.02),
        'ln2_b': nrm(ks[24], (L, D_MODEL), 0.02),
    }


def reference(x_prompt, x_sample, mem_prompt, state_ret, cache_mem_k, cache_mem_v,
              w_in, sgu_ln_g, sgu_ln_b, sgu_w, sgu_b, w_proj_a, ret_gn_g, w_proj_b,
              w_mem_k, w_mem_v, w_proj_c, w_out, ln1_g, ln1_b,
              w_ffn_gate, w_ffn_up, w_ffn_down, ln2_g, ln2_b):
    bp = x_prompt.shape[0]
    h_p, h_s = x_prompt, x_sample
    ret_p, mk_p_all, mv_p_all, ret_s, cv_s = [], [], [], [], []
    for l in range(DEPTH):
        p = {
            'w_in': w_in[l], 'sgu_ln_g': sgu_ln_g[l], 'sgu_ln_b': sgu_ln_b[l],
            'sgu_w': sgu_w[l], 'sgu_b': sgu_b[l], 'w_proj_a': w_proj_a[l],
            'ret_gn_g': ret_gn_g[l], 'w_proj_b': w_proj_b[l], 'w_proj_c': w_proj_c[l],
            'w_out': w_out[l], 'ln1_g': ln1_g[l], 'ln1_b': ln1_b[l],
            'w_ffn_gate': w_ffn_gate[l], 'w_ffn_up': w_ffn_up[l], 'w_ffn_down': w_ffn_down[l],
            'ln2_g': ln2_g[l], 'ln2_b': ln2_b[l],
        }
        mk_p = (mem_prompt @ w_mem_k[l]).reshape(bp, N_MEM, M_HEADS, M_HEAD_DIM)
        mv_p = (mem_prompt @ w_mem_v[l]).reshape(bp, N_MEM, M_HEADS, M_HEAD_DIM)
        s0 = jnp.zeros((bp, R_HEADS, R_QK_DIM, R_V_DIM), jnp.float32)
        h_p, s_p, _ = _layer(h_p, 0.0, s0, mk_p, mv_p, p)
        h_s, s_s, v_s = _layer(h_s, float(PAST_LEN), state_ret[l], cache_mem_k[l], cache_mem_v[l], p)
        ret_p.append(s_p)
        mk_p_all.append(mk_p)
        mv_p_all.append(mv_p)
        ret_s.append(s_s)
        cv_s.append(v_s)
    return (h_p, h_s, jnp.stack(ret_p), jnp.stack(mk_p_all), jnp.stack(mv_p_all), jnp.stack(ret_s), jnp.stack(cv_s))
```

```python
import contextlib
import os as _os
import math
import numpy as np
import concourse.bass as bass
import concourse.mybir as mybir
from concourse.bass_utils import run_bass_kernel_spmd

F32 = mybir.dt.float32
BF16 = mybir.dt.bfloat16
AF = mybir.ActivationFunctionType
ALU = mybir.AluOpType
AX = mybir.AxisListType

D = 2048
NPRE = 8
NOWN = 8
NT = NOWN + 1
NTOK = NT * 128
A_W = 1024
OFF_AU = 3 * D
OFF_AV = OFF_AU + 1024
OFF_RQ = OFF_AV + 1024
OFF_RK = OFF_RQ + 1024
OFF_RV = OFF_RK + 1024
OFF_RG = OFF_RV + 2048
OFF_MQ = OFF_RG + 2048
IN_W = OFF_MQ + 1024
D_FF = 5632
ALPHA = 2.0 ** 0.25
EPS = 1e-5
ARENA_WORDS = 53200
TG = [(0, 512), (512, 512), (1024, 128)]

COMPUTE = ('pe', 'act', 'dve', 'pool')
NOSELF = tuple(_os.environ.get('MK_NOSELF', 'pe').split(','))


class Inst:
    __slots__ = ('eng', 'fn', 'deps', 'need_inc', 'semval', 'is_dma', 'anchor', 'dma_val')

    def __init__(self, eng, fn, is_dma=False):
        self.eng = eng
        self.fn = fn
        self.deps = []
        self.need_inc = False
        self.semval = None
        self.is_dma = is_dma
        self.anchor = None
        self.dma_val = None


class Buf:
    def __init__(self, ap, name='', ghost=None):
        self.ap = ap
        self.name = name
        self.last_w = None
        self.readers = {}
        self.dma_readers = []
        self.ghost = list(ghost) if ghost else []
        self.dma_count = 0
        self.sem = None
        self.exclusive = False


class Prog:
    def __init__(self, nc):
        self.nc = nc
        self.streams = {e: [] for e in ('pe', 'act', 'dve', 'pool', 'sp')}
        self.anchors = []

    def _track(self, X, reads, writes):
        deps = []
        for b in reads:
            if b.last_w is not None:
                deps.append(b.last_w)
            if b.ghost:
                deps.extend(b.ghost)
            if b.exclusive:
                deps.extend(r for en, r in b.readers.items() if en != X.eng)
        for b in writes:
            if b.last_w is not None:
                deps.append(b.last_w)
            deps.extend(b.readers.values())
            deps.extend(b.dma_readers)
            if b.ghost:
                deps.extend(b.ghost)
                b.ghost = []
        seen = set()
        for d in deps:
            if d is X or id(d) in seen:
                continue
            seen.add(id(d))
            X.deps.append(d)
        for b in reads:
            if any(b is w for w in writes):
                continue
            if X.is_dma:
                b.dma_readers.append(X)
            else:
                b.readers[X.eng] = X
        for b in writes:
            b.last_w = X
            b.readers = {}
            b.dma_readers = []

    def op(self, eng, fn, reads=(), writes=()):
        X = Inst(eng, fn)
        self._track(X, list(reads), list(writes))
        self.streams[eng].append(X)
        return X

    def dma(self, queue, fn, anchor, reads=(), writes=()):
        X = Inst(queue, fn, is_dma=True)
        X.anchor = anchor
        anchor.dma_count += 16
        X.dma_val = anchor.dma_count
        if not any(a is anchor for a in self.anchors):
            self.anchors.append(anchor)
        self._track(X, list(reads), list(writes))
        self.streams[queue].append(X)
        return X

    def retire(self, bufs):
        g = []
        for b in bufs:
            if b.last_w is not None:
                g.append(b.last_w)
            g.extend(b.readers.values())
            g.extend(b.dma_readers)
            g.extend(b.ghost)
        return g

    def emit(self, block, semctx):
        for e, lst in self.streams.items():
            for X in lst:
                for d in X.deps:
                    if d.is_dma:
                        continue
                    if d.eng == X.eng and (not X.is_dma) and d.eng in NOSELF:
                        continue
                    d.need_inc = True
        esem = {e: semctx("eng_" + e) for e in COMPUTE}
        for e, lst in self.streams.items():
            c = 0
            for X in lst:
                if (not X.is_dma) and X.need_inc:
                    c += 1
                    X.semval = c
        for i, a in enumerate(self.anchors):
            a.sem = semctx("dma_%d" % i)
        final_waits = [(a.sem, a.dma_count) for a in self.anchors]

        def run_stream(e, handle, final=False):
            waited = {}
            for X in self.streams[e]:
                for d in X.deps:
                    if d.is_dma:
                        s, v = d.anchor.sem, d.dma_val
                    else:
                        if d.eng == e and (not X.is_dma) and d.eng in NOSELF:
                            continue
                        s, v = esem[d.eng], d.semval
                    k = id(s)
                    if waited.get(k, 0) >= v:
                        continue
                    waited[k] = v
                    handle.wait_ge(s, v)
                ins = X.fn(handle)
                if X.is_dma:
                    ins.then_inc(X.anchor.sem, 16)
                elif X.need_inc:
                    ins.then_inc(esem[e], 1)
            if final:
                for s, v in final_waits:
                    handle.wait_ge(s, v)

        @block.tensor
        def _(t):
            run_stream('pe', t)

        @block.scalar
        def _(a):
            run_stream('act', a)

        @block.vector
        def _(v):
            run_stream('dve', v)

        @block.gpsimd
        def _(g):
            run_stream('pool', g)

        @block.sync
        def _(s):
            run_stream('sp', s, final=True)


class Region:
    def __init__(self, arena, off, n, ghosts):
        self.arena = arena
        self.off = off
        self.n = n
        self.ghosts = ghosts
        self.bufs = []

    def f32(self, a, b):
        return self.arena.ap[:, self.off + a:self.off + b]

    def bf(self, a, b):
        return self.arena.ap[:, self.off + a:self.off + b].bitcast(BF16)

    def buf(self, ap, name=''):
        b = Buf(ap, name, ghost=self.ghosts)
        self.bufs.append(b)
        return b


class Arena:
    def __init__(self, P, ap, total):
        self.P = P
        self.ap = ap
        self.free = [[0, total, []]]

    def take(self, n, name=''):
        for i, (s, sz, g) in enumerate(self.free):
            if sz >= n:
                if sz == n:
                    self.free.pop(i)
                else:
                    self.free[i] = [s + n, sz - n, g]
                return Region(self, s, n, list(g))
        raise RuntimeError("arena full allocating %s (%d words); free=%s" % (name, n, [(s, z) for s, z, _ in self.free]))

    def take_split(self, n, name=''):
        regs = []
        left = n
        while left > 0:
            chunks = sorted(self.free, key=lambda r: -r[1])
            if not chunks:
                raise RuntimeError("arena full (split) allocating %s" % name)
            sz = min(chunks[0][1], left)
            for i, (s_, z_, g_) in enumerate(self.free):
                if s_ == chunks[0][0]:
                    if z_ == sz:
                        self.free.pop(i)
                    else:
                        self.free[i] = [s_ + sz, z_ - sz, g_]
                    regs.append(Region(self, s_, sz, list(g_)))
                    break
            left -= sz
        return regs

    def drop(self, reg):
        g = self.P.retire(reg.bufs) + list(reg.ghosts)
        self.free.append([reg.off, reg.n, g])
        self.free.sort(key=lambda r: r[0])
        merged = []
        for r in self.free:
            if merged and merged[-1][0] + merged[-1][1] == r[0]:
                merged[-1][1] += r[1]
                merged[-1][2] = merged[-1][2] + r[2]
            else:
                merged.append(r)
        self.free = merged


class Ring:
    def __init__(self, items):
        self.items = items
        self.i = 0

    def next(self):
        b = self.items[self.i % len(self.items)]
        self.i += 1
        return b


def build_program():
    nc = bass.Bass("TRN2", target_bir_lowering=False)

    def din(name, shape):
        return nc.dram_tensor(name, list(shape), F32, kind="ExternalInput").ap()

    def dout(name, shape):
        return nc.dram_tensor(name, list(shape), F32, kind="ExternalOutput").ap()

    x_own = din("x_own", [NOWN * 128, D])
    x_pre = din("x_pre", [NPRE * 128, D])
    x_smp = din("x_smp", [128, D])
    mem = din("mem", [256, D])
    state = din("state", [16, 8, 128, 256])
    ck = din("ck", [16, 256, 1024])
    cv = din("cv", [16, 256, 1024])
    w_in = din("w_in", [D, IN_W])
    w_pa = din("w_pa", [1024, D])
    w_pb = din("w_pb", [2048, D])
    w_pc = din("w_pc", [1024, D])
    w_mk = din("w_mk", [D, 1024])
    w_mv = din("w_mv", [D, 1024])
    w_out = din("w_out", [D, D])
    w_fg = din("w_fg", [D, D_FF])
    w_fu = din("w_fu", [D, D_FF])
    w_fd = din("w_fd", [D_FF, D])
    sgu_wT = din("sgu_wT", [128, 2, 4, 128])
    sgu_bb = din("sgu_bb", [1, 2 * 4 * 128])
    vec_a = din("vec_a", [1, 2 * 1024])
    vec_g = din("vec_g", [1, 2048])
    vec_ln = din("vec_ln", [1, 4 * 2048])
    cst_f = din("cst_f", [128, 17 * 128 * 2 + 16 + 16 + 16])
    cst_b = din("cst_b", [128, 128 * 3 + 16 * 128])

    y_own = dout("y_own", [NOWN * 128, D])
    y_smp = dout("y_smp", [128, D])
    st_p = dout("st_p", [8, 128, 256])
    mk_o = dout("mk_o", [256, 1024])
    mv_o = dout("mv_o", [256, 1024])
    st_s = dout("st_s", [16, 8, 128, 256])
    cv_s = dout("cv_s", [128, 1024])

    def xrows(t):
        if t < NOWN:
            return x_own[t * 128:(t + 1) * 128, :]
        return x_smp[:, :]

    def yrows(t):
        if t < NOWN:
            return y_own[t * 128:(t + 1) * 128, :]
        return y_smp[:, :]

    def wcols(w, c0, n, k0=0, nk=None):
        nk = (w.shape[0] // 128 - k0) if nk is None else nk
        return w[k0 * 128:(k0 + nk) * 128, c0:c0 + n].rearrange("(k p) n -> p k n", p=128)

    with contextlib.ExitStack() as es:
        arena_t = es.enter_context(nc.sbuf_tensor("arena", [128, ARENA_WORDS], F32))
        pst = [es.enter_context(nc.psum_tensor("ps%d" % i, [128, 512], F32)) for i in range(8)]
        semctx = lambda name: es.enter_context(nc.semaphore(name))
        block = es.enter_context(nc.Block())
        P = Prog(nc)
        A = Arena(P, arena_t[:, :], ARENA_WORDS)
        _stop = _os.environ.get("MK_STOP", "")
        PS = [Buf(p[:, :], "ps%d" % i) for i, p in enumerate(pst)]
        for b_ in PS:
            b_.exclusive = True
        psring = Ring(PS)

        def psb(b):
            return b.ap

        evac_flip = [0]

        def evac(dst_ap, src_ap, reads, writes, eng=None):
            if eng is None:
                eng = 'act' if (evac_flip[0] % 2 == 0) else 'dve'
                evac_flip[0] += 1
            if eng == 'act':
                P.op('act', lambda e: e.activation(dst_ap, src_ap, AF.Copy), reads=reads, writes=writes)
            else:
                P.op('dve', lambda e: e.tensor_copy(dst_ap, src_ap), reads=reads, writes=writes)

        def mm(out_b, out_ap, l_b, l_ap, r_b, r_ap, start, stop):
            P.op('pe', lambda e: e.matmul(out_ap, l_ap, r_ap, start=start, stop=stop),
                 reads=[l_b, r_b], writes=[out_b])

        def tr(out_b, out_ap, in_b, in_ap, ident_b):
            P.op('pe', lambda e: e.transpose(out_ap, in_ap, ident_b.ap), reads=[in_b, ident_b], writes=[out_b])

        def load(queue, dst_b, dst_ap, src_ap):
            P.dma(queue, lambda e: e.dma_start(out=dst_ap, in_=src_ap), dst_b, writes=[dst_b])

        def store(queue, dst_ap, src_b, src_ap):
            P.dma(queue, lambda e: e.dma_start(out=dst_ap, in_=src_ap), src_b, reads=[src_b])

        def tt(eng, out_ap, a_ap, b_ap, op, reads, writes):
            P.op(eng, lambda e: e.tensor_tensor(out_ap, a_ap, b_ap, op), reads=reads, writes=writes)

        def ts(eng, out_ap, a_ap, s1, s2, op0, op1, reads, writes):
            if op1 is None:
                P.op(eng, lambda e: e.tensor_scalar(out_ap, a_ap, s1, None, op0), reads=reads, writes=writes)
            else:
                P.op(eng, lambda e: e.tensor_scalar(out_ap, a_ap, s1, s2, op0, op1), reads=reads, writes=writes)

        def stt(out_ap, a_ap, s, b_ap, op0, op1, reads, writes):
            P.op('dve', lambda e: e.scalar_tensor_tensor(out_ap, a_ap, s, b_ap, op0, op1), reads=reads, writes=writes)

        def act(out_ap, in_ap, func, reads, writes, bias=None, scale=None, accum=None):
            kw = {}
            if bias is not None:
                kw['bias'] = bias
            if scale is not None:
                kw['scale'] = scale
            if accum is not None:
                kw['accum_out'] = accum
            P.op('act', lambda e: e.activation(out_ap, in_ap, func, **kw), reads=reads, writes=writes)

        r_eps = A.take(2, "epsb")
        EPSB = r_eps.buf(r_eps.f32(0, 2), "epsb")
        P.op('dve', lambda e: e.memset(EPSB.ap, EPS), writes=[EPSB])
        r_cs = A.take(48, "cst_s")
        CS = r_cs.buf(r_cs.f32(0, 48), "cst_s")
        o0 = 2 * 17 * 128
        load('sp', CS, CS.ap, cst_f[:, o0:o0 + 48])
        qscale = CS.ap[:, 0:16]
        kscale = CS.ap[:, 16:32]
        rowmask = CS.ap[:, 32:48]
        cst_cos = cst_f[:, 0:17 * 128].rearrange("p (t d) -> p t d", t=17)
        cst_sin = cst_f[:, 17 * 128:2 * 17 * 128].rearrange("p (t d) -> p t d", t=17)
        NCB = (128 * 3 + 16 * 128) // 2
        r_cb = A.take(NCB, "cst_b")
        CB = r_cb.buf(r_cb.bf(0, NCB), "cst_b")
        load('pool', CB, CB.ap, cst_b[:, :])
        identb = CB
        ident_ap = CB.ap[:, 0:128]
        maskT_p = CB.ap[:, 128:256]
        maskT_s = CB.ap[:, 256:384]
        blockmask = CB.ap[:, 384:384 + 2048].rearrange("p (b i) -> p b i", b=16)

        def trn(out_b, out_ap, in_b, in_ap):
            P.op('pe', lambda e: e.matmul(out_ap, in_ap, ident_ap, start=True, stop=True), reads=[in_b, identb], writes=[out_b])

        r_xP = A.take(16 * NPRE * 128 // 2, "xTpre")
        xP_ap = r_xP.bf(0, 16 * NPRE * 128 // 2).rearrange("p (k n) -> p k n", k=16)
        xP = [r_xP.buf(xP_ap[:, :, t * 128:(t + 1) * 128], "xP%d" % t) for t in range(NPRE)]
        r_xin = A.take(8 * 1024, "xin")
        xin = Ring([r_xin.buf(r_xin.bf(i * 1024, (i + 1) * 1024), "xin%d" % i) for i in range(8)])

        def load_transpose(src_rows, dstbuf, xi=None):
            if xi is None:
                xi = xin.next()
                load('pool', xi, xi.ap, src_rows)
            for hf in range(4):
                ps = psring.next()
                pv = psb(ps)
                for k in range(4):
                    kk = hf * 4 + k
                    trn(ps, pv[:, k * 128:(k + 1) * 128], xi, xi.ap[:, kk * 128:(kk + 1) * 128])
                evac(dstbuf.ap[:, hf * 4:(hf + 1) * 4, :], pv.rearrange("p (k n) -> p k n", k=4), [ps], [dstbuf])

        for t in range(NPRE):
            load_transpose(x_pre[t * 128:(t + 1) * 128, :], xP[t])
        r_xin2 = A.take(NT * 1024, "xin_own")
        xin_own = [r_xin2.buf(r_xin2.bf(i * 1024, (i + 1) * 1024), "xino%d" % i) for i in range(NT)]

        ROT_ENG = _os.environ.get("MK_ROT", "dve")

        def rotary(qf, nqk, cosb, cos_ap, sin_ap, t1, t2, qb):
            cs = cos_ap.to_broadcast([128, nqk, 128])
            sa = sin_ap[:, :, 0:64].to_broadcast([128, nqk, 64])
            sb_ = sin_ap[:, :, 64:128].to_broadcast([128, nqk, 64])
            tt(ROT_ENG, t1.ap[:, 0:nqk, :], qf.ap[:, 0:nqk, :], cs, ALU.mult, [qf, cosb], [t1])
            tt(ROT_ENG, t2.ap[:, 0:nqk, 0:64], qf.ap[:, 0:nqk, 64:128], sa, ALU.mult, [qf, cosb], [t2])
            tt(ROT_ENG, t2.ap[:, 0:nqk, 64:128], qf.ap[:, 0:nqk, 0:64], sb_, ALU.mult, [qf, cosb], [t2])
            tt(ROT_ENG, qb.ap[:, 0:nqk, :], t1.ap[:, 0:nqk, :], t2.ap[:, 0:nqk, :], ALU.add, [t1, t2], [qb])

        if _stop == "0a":
            P.emit(block, semctx)
            return nc
        def run_pipeline(items, stages, pre=None, post=None):
            maxd = max(d for d, _ in stages)
            for step in range(len(items) + maxd):
                if pre:
                    pre(step)
                for d, fn in stages:
                    i = step - d
                    if 0 <= i < len(items):
                        fn(items[i])
                if post:
                    post(step)

        r_T = A.take(2048, "Tall")
        Tall = r_T.buf(r_T.f32(0, 2048).rearrange("p (h e) -> p h e", h=8), "Tall")
        P.op('dve', lambda e: e.memset(Tall.ap, 0.0), writes=[Tall])
        r_cp = A.take(2 * 8 * 128, "cfpre")
        CFP = r_cp.buf(r_cp.f32(0, 2048).rearrange("p (w t d) -> p w t d", w=2, t=8), "cfpre")
        load('sp', CFP, CFP.ap[:, 0, :, :], cst_cos[:, 0:8, :])
        load('sp', CFP, CFP.ap[:, 1, :, :], cst_sin[:, 0:8, :])
        r_wkv = A.take(2 * 3072, "wkv")
        wkvr = Ring([r_wkv.buf(r_wkv.bf(i * 3072, (i + 1) * 3072).rearrange("p (k n) -> p k n", k=16), "wkv%d" % i)
                     for i in range(2)])
        r_b0 = A.take(2 * 128 + 128 + 128 + 3 * 64 + 3 * 128, "b0work")
        kf0 = Ring([r_b0.buf(r_b0.f32(i * 128, (i + 1) * 128).rearrange("p (j d) -> p j d", j=1), "kf0_%d" % i) for i in range(2)])
        t10 = r_b0.buf(r_b0.f32(256, 384).rearrange("p (j d) -> p j d", j=1), "t10")
        t20 = r_b0.buf(r_b0.f32(384, 512).rearrange("p (j d) -> p j d", j=1), "t20")
        kb0 = Ring([r_b0.buf(r_b0.bf(512 + i * 64, 512 + (i + 1) * 64).rearrange("p (j d) -> p j d", j=1), "kb0_%d" % i) for i in range(3)])
        vb0 = Ring([r_b0.buf(r_b0.bf(704 + i * 128, 704 + (i + 1) * 128), "vb0_%d" % i) for i in range(3)])
        r_wh0 = A.take(6144, "wh0")
        r_wh1 = A.take(6144, "wh1")
        whs = [r_wh0.buf(r_wh0.bf(0, 6144).rearrange("p (k n) -> p k n", k=16), "wh0"),
               r_wh1.buf(r_wh1.bf(0, 6144).rearrange("p (k n) -> p k n", k=16), "wh1")]

        def load_wh(h):
            wh = whs[h % 2]
            load('pool', wh, wh.ap[:, :, 0:128], wcols(w_in, OFF_RQ + h * 128, 128))
            load('pool', wh, wh.ap[:, :, 128:256], wcols(w_in, OFF_RK + h * 128, 128))
            load('pool', wh, wh.ap[:, :, 256:512], wcols(w_in, OFF_RV + h * 256, 256))
            load('pool', wh, wh.ap[:, :, 512:768], wcols(w_in, OFF_RG + h * 256, 256))

        ringA0 = Ring([PS[0], PS[1], PS[2], PS[3]])
        ringD0 = Ring([PS[4], PS[5], PS[6], PS[7]])
        b0ctx = {}
        b0w = {}

        def b0_pre(step):
            if step == 2:
                for t_ in range(NT):
                    load('pool', xin_own[t_], xin_own[t_].ap, xrows(t_))
            if step == 36:
                load_wh(0)
            if step == 56:
                load_wh(1)
            if step % NPRE == 0 and step // NPRE < 8:
                h = step // NPRE
                wkv = wkvr.next()
                load('pool', wkv, wkv.ap[:, :, 0:128], wcols(w_in, OFF_RK + h * 128, 128))
                load('pool', wkv, wkv.ap[:, :, 128:384], wcols(w_in, OFF_RV + h * 256, 256))
                b0w[h] = wkv

        def b0_s0(it):
            h, t = it
            wkv = b0w[h]
            psA = ringA0.next()
            for k in range(16):
                mm(psA, psA.ap[:, 0:384], xP[t], xP[t].ap[:, k, :], wkv, wkv.ap[:, k, :], k == 0, k == 15)
            kf = kf0.next()
            act(kf.ap[:, 0, :], psA.ap[:, 0:128], AF.Copy, [psA, CS], [kf], scale=kscale[:, h:h + 1])
            vb = vb0.next()
            evac(vb.ap, psA.ap[:, 128:384], [psA], [vb], eng='act')
            kb = kb0.next()
            rotary(kf, 1, CFP, CFP.ap[:, 0, t:t + 1, :], CFP.ap[:, 1, t:t + 1, :], t10, t20, kb)
            b0ctx[it] = (kb, vb)

        def b0_s1(it):
            h, t = it
            kb, vb = b0ctx.pop(it)
            c_p = math.exp(128.0 * math.log1p(-(2.0 ** (-5.0 - h))))
            psD = ringD0.next()
            mm(psD, psD.ap[:, 0:256], kb, kb.ap[:, 0, :], vb, vb.ap, True, True)
            stt(Tall.ap[:, h, :], Tall.ap[:, h, :], c_p, psD.ap[:, 0:256], ALU.mult, ALU.add, [Tall, psD], [Tall])

        run_pipeline([(h, t) for h in range(8) for t in range(NPRE)], [(0, b0_s0), (1, b0_s1)], pre=b0_pre)
        A.drop(r_b0)
        A.drop(r_wkv)
        A.drop(r_cp)
        A.drop(r_xP)

        if _stop == "B0":
            P.emit(block, semctx)
            return nc
        r_xT = A.take(16 * NTOK // 2, "xT")
        xT_ap = r_xT.bf(0, 16 * NTOK // 2).rearrange("p (k n) -> p k n", k=16)
        xT = [r_xT.buf(xT_ap[:, :, t * 128:(t + 1) * 128], "xT%d" % t) for t in range(NT)]
        for t in range(NT):
            load_transpose(None, xT[t], xi=xin_own[t])
        A.drop(r_xin)
        A.drop(r_xin2)

        if _stop == "0b":
            P.emit(block, semctx)
            return nc
        r_bT = A.take(16 * NTOK // 2, "bT")
        bT_ap = r_bT.bf(0, 16 * NTOK // 2).rearrange("p (k n) -> p k n", k=16)
        bT = [r_bT.buf(bT_ap[:, :, t * 128:(t + 1) * 128], "bT%d" % t) for t in range(NT)]
        r_co = A.take(2 * 9 * 128, "cfown")
        CFO = r_co.buf(r_co.f32(0, 2304).rearrange("p (w t d) -> p w t d", w=2, t=9), "cfown")
        load('sp', CFO, CFO.ap[:, 0, :, :], cst_cos[:, 8:17, :])
        load('sp', CFO, CFO.ap[:, 1, :, :], cst_sin[:, 8:17, :])
        r_gng = A.take(2 * 256, "gng")
        gngr = Ring([r_gng.buf(r_gng.f32(i * 256, (i + 1) * 256), "gng%d" % i) for i in range(2)])
        r_s0 = A.take(4096, "s0")
        s0 = r_s0.buf(r_s0.f32(0, 4096).rearrange("p (b e) -> p b e", b=16), "s0")
        r_bs = A.take_split(12100, "bwork")
        curs = [0 for _ in r_bs]

        def bw(nwords, dtype, name, shape=None, **kw):
            for i_, r_b in enumerate(r_bs):
                if curs[i_] + nwords <= r_b.n:
                    a0 = curs[i_]
                    curs[i_] += nwords
                    ap = r_b.f32(a0, a0 + nwords) if dtype == F32 else r_b.bf(a0, a0 + nwords)
                    if shape:
                        ap = ap.rearrange(shape, **kw)
                    return r_b.buf(ap, name)
            raise RuntimeError("bwork full at %s" % name)

        s0bf = bw(2048, BF16, "s0bf", "p (b e) -> p b e", b=16)
        qmk = bw(1024, BF16, "qmaskT", "p (b i) -> p b i", b=16)
        kmk = bw(1024, BF16, "kmaskR", "p (b d) -> p b d", b=16)
        qk_f = Ring([bw(256, F32, "qkf%d" % i, "p (j d) -> p j d", j=2) for i in range(2)])
        t1_f = bw(256, F32, "t1f", "p (j d) -> p j d", j=2)
        t2_f = bw(256, F32, "t2f", "p (j d) -> p j d", j=2)
        qk_b = Ring([bw(128, BF16, "qkb%d" % i, "p (j d) -> p j d", j=2) for i in range(7)])
        qkT = Ring([bw(128, BF16, "qkT%d" % i, "p (j d) -> p j d", j=2) for i in range(5)])
        v_b = Ring([bw(128, BF16, "vb%d" % i) for i in range(7)])
        scm = Ring([bw(64, BF16, "scm%d" % i) for i in range(3)])
        S_bf = Ring([bw(128, BF16, "Sbf%d" % i) for i in range(2)])
        sg_f = [bw(256, F32, "sgf%d" % t) for t in range(NT)]
        gst = Ring([bw(12, F32, "gst%d" % i) for i in range(2)])
        on_f = bw(256, F32, "onf")
        bg_b = Ring([bw(128, BF16, "bgb%d" % i) for i in range(4)])
        stmp = bw(512, F32, "stmp", "p (b e) -> p b e", b=2)
        vb_smp = bw(128, BF16, "vbsmp")
        deferred = []

        BD = [int(x) for x in _os.environ.get('MK_BD', '1,2,3,4').split(',')]
        ringA = Ring([PS[0], PS[1]])
        ringT = Ring([PS[2], PS[3], PS[4]])
        ringO = Ring([PS[5], PS[6]])
        ringD = Ring([PS[7]])
        ringDS = Ring([PS[7], PS[5], PS[6]])
        bctx = {}
        bhead = {}
        Sbcur = [None]

        def hconst(h):
            lg = math.log1p(-(2.0 ** (-5.0 - h)))
            return math.exp(128.0 * lg), math.exp(8.0 * lg)

        def b_pre(step):
            if step % NT == 0 and step // NT < 8:
                h = step // NT
                if h >= 2:
                    load_wh(h)
                bhead[h] = whs[h % 2]
            if step == NT * 7 + 4:
                load('pool', whs[0], whs[0].ap[:, :, 0:768], wcols(w_in, OFF_AV, 768))
            if step % NT == min(BD[2], NT - 1) and step // NT < 8:
                h = step // NT
                c_p, c_s = hconst(h)
                wh = bhead[h]
                gng = gngr.next()
                load('sp', gng, gng.ap, vec_g[0:1, h * 256:(h + 1) * 256].partition_broadcast(128))
                Sb = S_bf.next()
                act(Sb.ap, Tall.ap[:, h, :], AF.Copy, [Tall], [Sb], scale=c_p)
                Sbcur[0] = Sb
                for t in range(NT):
                    psG = ringA.next()
                    for k in range(16):
                        mm(psG, psG.ap[:, 0:256], xT[t], xT[t].ap[:, k, :], wh, wh.ap[:, k, 512:768], k == 0, k == 15)
                    act(sg_f[t].ap, psG.ap[:, 0:256], AF.Silu, [psG], [sg_f[t]])
                    tt(ROT_ENG, sg_f[t].ap, sg_f[t].ap, gng.ap, ALU.mult, [sg_f[t], gng], [sg_f[t]])

        def b_post(step):
            if step == 0:
                load('sp', s0, s0.ap, state[:, 0, :, :].rearrange("b d e -> d b e"))
            if deferred:
                deferred.pop(0)()

        def b_s0(it):
            h, t = it
            wh = bhead[h]
            smp = (t == NOWN)
            xb = xT[t]
            psA = ringA.next()
            for k in range(16):
                mm(psA, psA.ap[:, 0:512], xb, xb.ap[:, k, :], wh, wh.ap[:, k, 0:512], k == 0, k == 15)
            which = 8 if smp else 0
            qf = qk_f.next()
            act(qf.ap[:, 0, :], psA.ap[:, 0:128], AF.Copy, [psA, CS], [qf], scale=qscale[:, which + h:which + h + 1])
            act(qf.ap[:, 1, :], psA.ap[:, 128:256], AF.Copy, [psA, CS], [qf], scale=kscale[:, which + h:which + h + 1])
            vb = vb_smp if smp else v_b.next()
            evac(vb.ap, psA.ap[:, 256:512], [psA], [vb], eng='act')
            qb = qk_b.next()
            if _os.environ.get("MK_ABL", "") == "rot":
                evac(qb.ap, qf.ap, [qf], [qb], eng='act')
            else:
                rotary(qf, 2, CFO, CFO.ap[:, 0, t:t + 1, :], CFO.ap[:, 1, t:t + 1, :], t1_f, t2_f, qb)
            bctx[it] = dict(vb=vb, qb=qb)

        def b_s1(it):
            c = bctx[it]
            qb = c['qb']
            psT = ringT.next()
            pv = psb(psT)
            trn(psT, pv[:, 0:128], qb, qb.ap[:, 0, :])
            trn(psT, pv[:, 128:256], qb, qb.ap[:, 1, :])
            qT = qkT.next()
            evac(qT.ap, pv[:, 0:256].rearrange("p (j d) -> p j d", j=2), [psT], [qT], eng='act')
            c['qT'] = qT

        def b_s2(it):
            h, t = it
            c = bctx[it]
            qT = c['qT']
            smp = (t == NOWN)
            psS = ringT.next()
            mm(psS, psS.ap[:, 0:128], qT, qT.ap[:, 1, :], qT, qT.ap[:, 0, :], True, True)
            sm = scm.next()
            tt('dve', sm.ap, psS.ap[:, 0:128], maskT_s if smp else maskT_p, ALU.mult, [psS, CB], [sm])
            c['sm'] = sm
            if smp:
                tt('dve', qmk.ap, qT.ap[:, 0:1, :].to_broadcast([128, 16, 128]), blockmask, ALU.mult, [qT, CB], [qmk])
                evac(s0bf.ap, s0.ap, [s0], [s0bf], eng='act')

        def b_s3(it):
            h, t = it
            c = bctx[it]
            qT, sm, vb, qb = c['qT'], c['sm'], c['vb'], c['qb']
            smp = (t == NOWN)
            c_p, c_s = hconst(h)
            psO = ringO.next()
            if not smp:
                Sb = Sbcur[0]
                mm(psO, psO.ap[:, 0:256], sm, sm.ap, vb, vb.ap, True, False)
                mm(psO, psO.ap[:, 0:256], qT, qT.ap[:, 0, :], Sb, Sb.ap, False, True)
                psD = ringD.next()
                mm(psD, psD.ap[:, 0:256], qb, qb.ap[:, 1, :], vb, vb.ap, True, True)
                stt(Tall.ap[:, h, :], Tall.ap[:, h, :], c_p, psD.ap[:, 0:256], ALU.mult, ALU.add, [Tall, psD], [Tall])
                if t < NOWN - 1:
                    Sb = S_bf.next()
                    act(Sb.ap, Tall.ap[:, h, :], AF.Copy, [Tall], [Sb], scale=c_p)
                    Sbcur[0] = Sb
                else:
                    act(Tall.ap[:, h, :], Tall.ap[:, h, :], AF.Copy, [Tall], [Tall], scale=c_p)
            else:
                mm(psO, psO.ap[:, 0:256], sm, sm.ap, vb, vb.ap, True, False)
                for b in range(16):
                    mm(psO, psO.ap[:, 0:256], qmk, qmk.ap[:, b, :], s0bf, s0bf.ap[:, b, :], False, b == 15)
            _abl = _os.environ.get("MK_ABL", "")
            if _abl == "gn":
                bg = bg_b.next()
                tt('dve', bg.ap, psO.ap[:, 0:256], sg_f[t].ap, ALU.mult, [psO, sg_f[t]], [bg])
                c['bg'] = bg
                return
            gs_ = gst.next()
            P.op('dve', lambda e, a=gs_.ap[:, 0:6], b=psO.ap[:, 0:256]: e.bn_stats(a, b), reads=[psO], writes=[gs_])
            P.op('dve', lambda e, a=gs_.ap[:, 6:8], b=gs_.ap[:, 0:6]: e.bn_aggr(a, b), reads=[gs_], writes=[gs_])
            act(gs_.ap[:, 9:10], gs_.ap[:, 7:8], AF.Sqrt, [gs_, EPSB], [gs_], bias=EPSB.ap[:, 0:1])
            P.op('dve', lambda e, a=gs_.ap[:, 10:11], b=gs_.ap[:, 9:10]: e.reciprocal(a, b), reads=[gs_], writes=[gs_])
            ts('dve', on_f.ap, psO.ap[:, 0:256], gs_.ap[:, 6:7], gs_.ap[:, 10:11], ALU.subtract, ALU.mult, [psO, gs_], [on_f])
            bg = bg_b.next()
            tt('dve', bg.ap, on_f.ap, sg_f[t].ap, ALU.mult, [on_f, sg_f[t]], [bg])
            c['bg'] = bg

        def b_s4(it):
            h, t = it
            c = bctx.pop(it)
            bg, vb, qb = c['bg'], c['vb'], c['qb']
            smp = (t == NOWN)
            c_p, c_s = hconst(h)
            psT2 = ringT.next()
            pv2 = psb(psT2)
            trn(psT2, pv2[:, 0:128], bg, bg.ap[:, 0:128])
            trn(psT2, pv2[:, 128:256], bg, bg.ap[:, 128:256])
            evac(bT[t].ap[:, 2 * h:2 * h + 2, :], pv2[:, 0:256].rearrange("p (j d) -> p j d", j=2), [psT2], [bT[t]], eng='act')
            if smp:
                tt('dve', kmk.ap, qb.ap[:, 1:2, :].to_broadcast([128, 16, 128]),
                   rowmask.unsqueeze(2).to_broadcast([128, 16, 128]), ALU.mult, [qb, CS], [kmk])
                def chunk(bps, last, h=h, vb=vb, c_s=c_s):
                    for bp in bps:
                        psD = ringDS.next()
                        for j in range(2):
                            b = bp * 2 + j
                            mm(psD, psD.ap[:, j * 256:(j + 1) * 256], kmk, kmk.ap[:, b, :], vb, vb.ap, True, True)
                        tt('dve', stmp.ap, psD.ap.rearrange("p (b e) -> p b e", b=2), s0.ap[:, bp * 2:bp * 2 + 2, :],
                           ALU.add, [psD, s0], [stmp])
                        act(s0.ap[:, bp * 2:bp * 2 + 2, :], stmp.ap, AF.Copy, [stmp], [s0], scale=c_s)
                    if last:
                        store('sp', st_s[:, h, :, :].rearrange("b d e -> d b e"), s0, s0.ap)
                        if h + 1 < 8:
                            load('sp', s0, s0.ap, state[:, h + 1, :, :].rearrange("b d e -> d b e"))
                chunk(list(range(8)), True)

        run_pipeline([(h, t) for h in range(8) for t in range(NT)],
                     [(0, b_s0), (BD[0], b_s1), (BD[1], b_s2), (BD[2], b_s3), (BD[3], b_s4)], pre=b_pre, post=b_post)
        while deferred:
            deferred.pop(0)()
        store('sp', st_p.rearrange("h d e -> d h e"), Tall, Tall.ap)
        for r_ in r_bs:
            A.drop(r_)
        A.drop(r_wh1)
        A.drop(r_s0)
        A.drop(r_gng)
        A.drop(r_co)
        A.drop(r_T)
        A.drop(r_cs)

        if _stop == "B":
            P.emit(block, semctx)
            return nc
        r_aT = A.take(8 * NTOK // 2, "aT")
        aT_ap = r_aT.bf(0, 8 * NTOK // 2).rearrange("p (k n) -> p k n", k=8)
        aT = r_aT.buf(aT_ap, "aT")
        WVa = whs[0]
        r_wvb = A.take(2048, "wvb")
        WVb = r_wvb.buf(r_wvb.bf(0, 2048).rearrange("p (k n) -> p k n", k=16), "wvb")
        load('pool', WVb, WVb.ap, wcols(w_in, OFF_AV + 768, 256))
        r_at = A.take(2048 + 1024 + 1024 + 512, "atab")
        LNA = r_at.buf(r_at.f32(0, 2048), "lna")
        load('sp', LNA, LNA.ap, vec_a[0:1, :].partition_broadcast(128))
        BSB = r_at.buf(r_at.f32(2048, 3072).rearrange("p (w g i) -> p w g i", w=2, g=4), "bsb")
        load('sp', BSB, r_at.f32(2048, 3072), sgu_bb[0:1, :].partition_broadcast(128))
        WSF = r_at.buf(r_at.f32(3072, 4096).rearrange("p (w g i) -> p w g i", w=2, g=4), "wsf")
        load('sp', WSF, r_at.f32(3072, 4096), sgu_wT.rearrange("j w g i -> j (w g i)"))
        WSB = r_at.buf(r_at.bf(4096, 4608).rearrange("p (w g i) -> p w g i", w=2, g=4), "wsb")
        tt('dve', WSB.ap[:, 0, :, :], WSF.ap[:, 0, :, :], maskT_p.unsqueeze(1).to_broadcast([128, 4, 128]), ALU.mult,
           [WSF, CB], [WSB])
        tt('dve', WSB.ap[:, 1, :, :], WSF.ap[:, 1, :, :], maskT_s.unsqueeze(1).to_broadcast([128, 4, 128]), ALU.mult,
           [WSF, CB], [WSB])
        r_zTs = [A.take(4 * NTOK, "zT%d" % i) for i in range(2)]
        zT_aps = [r_.f32(0, 4 * NTOK).rearrange("p (k n) -> p k n", k=4) for r_ in r_zTs]
        zTh = [[r_zTs[i].buf(zT_aps[i][:, :, t * 128:(t + 1) * 128], "zT%d_%d" % (i, t)) for t in range(NT)] for i in range(2)]
        r_aw = A.take(2 * 1024 + 2 * 512 + 64 + 512, "awork")
        vg_items = []
        for i in range(2):
            full = Buf(r_aw.f32(i * 1024, (i + 1) * 1024), "vgfull%d" % i)
            parts = [r_aw.buf(r_aw.f32(i * 1024 + g * 256, i * 1024 + (g + 1) * 256), "vg%d_%d" % (i, g)) for g in range(4)]
            vg_items.append((full, parts))
        vg = Ring(vg_items)
        vlb = Ring([r_aw.buf(r_aw.bf(2048 + i * 512, 2048 + (i + 1) * 512), "vlb%d" % i) for i in range(2)]
                   + [r_aw.buf(r_aw.bf(3136, 3648), "vlb2")])
        ast = Buf(r_aw.f32(3072, 3072 + 24).rearrange("p (g s) -> p g s", g=4), "ast")
        amv = Buf(r_aw.f32(3072 + 24, 3072 + 32).rearrange("p (g s) -> p g s", g=4), "amv")
        ast_g = [r_aw.buf(r_aw.f32(3072 + 6 * g, 3072 + 6 * g + 6), "ast%d" % g) for g in range(4)]
        amv_g = [r_aw.buf(r_aw.f32(3072 + 24 + 2 * g, 3072 + 26 + 2 * g), "amv%d" % g) for g in range(4)]
        atm = r_aw.buf(r_aw.f32(3072 + 32, 3072 + 36), "atm")
        ars = r_aw.buf(r_aw.f32(3072 + 36, 3072 + 40), "ars")
        anm = r_aw.buf(r_aw.f32(3072 + 40, 3072 + 44), "anm")
        actx = {}

        def a_s0(t):
            pss = [psring.next(), psring.next()]
            for k in range(16):
                mm(pss[0], pss[0].ap, xT[t], xT[t].ap[:, k, :], WVa, WVa.ap[:, k, 0:512], k == 0, k == 15)
            for k in range(16):
                mm(pss[1], pss[1].ap[:, 0:256], xT[t], xT[t].ap[:, k, :], WVa, WVa.ap[:, k, 512:768], k == 0, k == 15)
            for k in range(16):
                mm(pss[1], pss[1].ap[:, 256:512], xT[t], xT[t].ap[:, k, :], WVb, WVb.ap[:, k, :], k == 0, k == 15)
            v, vgs = vg.next()
            for n in range(2):
                act(v.ap[:, n * 512:(n + 1) * 512], pss[n].ap, AF.Gelu_apprx_tanh, [pss[n]], [vgs[2 * n], vgs[2 * n + 1]])
            for g in range(4):
                P.op('dve', lambda e, a=ast.ap[:, g, :], b=v.ap[:, g * 256:(g + 1) * 256]: e.bn_stats(a, b),
                     reads=[vgs[g]], writes=[ast_g[g]])
            for g in range(4):
                P.op('dve', lambda e, a=amv.ap[:, g, :], b=ast.ap[:, g, :]: e.bn_aggr(a, b), reads=[ast_g[g]], writes=[amv_g[g]])
            act(atm.ap, amv.ap[:, :, 1], AF.Sqrt, amv_g + [EPSB], [atm], bias=EPSB.ap[:, 0:1])
            P.op('dve', lambda e: e.reciprocal(ars.ap, atm.ap), reads=[atm], writes=[ars])
            stt(anm.ap, amv.ap[:, :, 0], -1.0, ars.ap, ALU.mult, ALU.mult, amv_g + [ars], [anm])
            for g in range(4):
                act(v.ap[:, g * 256:(g + 1) * 256], v.ap[:, g * 256:(g + 1) * 256], AF.Identity, [vgs[g], ars, anm], [vgs[g]],
                    bias=anm.ap[:, g:g + 1], scale=ars.ap[:, g:g + 1])
            tt('dve', v.ap, v.ap, LNA.ap[:, 0:1024], ALU.mult, vgs + [LNA], vgs)
            tt('dve', v.ap, v.ap, LNA.ap[:, 1024:2048], ALU.add, vgs + [LNA], vgs)
            if t == NOWN:
                P.dma('sp', lambda e: e.dma_start(out=cv_s[:, :], in_=v.ap), vgs[0], reads=vgs)
            vb = vlb.next()
            evac(vb.ap, v.ap, vgs, [vb], eng='act')
            actx[t] = vb

        def a_s1(t):
            w = 1 if t == NOWN else 0
            vb = actx.pop(t)
            for hf in range(2):
                ps = psring.next()
                for cc in range(4):
                    c = hf * 4 + cc
                    mm(ps, ps.ap[:, cc * 128:(cc + 1) * 128], vb, vb.ap[:, c * 128:(c + 1) * 128], WSB, WSB.ap[:, w, c // 2, :],
                       True, True)
                tt('dve', zTh[hf][t].ap.rearrange("p (g j) i -> p g j i", g=2),
                   ps.ap.rearrange("p (g j i) -> p g j i", g=2, j=2),
                   BSB.ap[:, w, hf * 2:hf * 2 + 2, :].unsqueeze(2).to_broadcast([128, 2, 2, 128]),
                   ALU.add, [ps, BSB], [zTh[hf][t]])

        run_pipeline(list(range(NT)), [(0, a_s0), (1, a_s1)])
        A.drop(r_wvb)
        A.drop(r_wh0)
        r_memin = A.take(2048, "memin")
        memin = r_memin.buf(r_memin.bf(0, 2048).rearrange("p (t f) -> p t f", t=2), "memin")
        load('pool', memin, memin.ap, mem.rearrange("(t p) f -> p t f", p=128))
        r_wk0 = A.take(4096, "wk0")
        wk0 = r_wk0.buf(r_wk0.bf(0, 4096).rearrange("p (k n) -> p k n", k=16), "wkp0")
        load('pool', wk0, wk0.ap, wcols(w_mk, 0, 512))
        r_wus = [A.take(1024, "wu%d" % i) for i in range(3)]
        wur = Ring([r_wus[i].buf(r_wus[i].bf(0, 1024).rearrange("p (k n) -> p k n", k=16), "wu%d" % i) for i in range(3)])
        r_ugs = [A.take(512, "ug%d" % i) for i in range(2)]
        ugr = Ring([r_ugs[i].buf(r_ugs[i].f32(0, 512), "ug%d" % i) for i in range(2)])

        def xT_rd(o, n):
            return [xT[t] for t in range(o // 128, (o + n) // 128)]

        def tl(lst, o, n):
            return [lst[t] for t in range(o // 128, (o + n) // 128)]

        for c in range(8):
            wu = wur.next()
            load('pool', wu, wu.ap, wcols(w_in, OFF_AU + c * 128, 128))
            for (o, n) in TG:
                ps = psring.next()
                for k in range(16):
                    P.op('pe', lambda e, a=ps.ap[:, 0:n], l=wu.ap[:, k, :], r=xT_ap[:, k, o:o + n], s=(k == 0), p=(k == 15):
                         e.matmul(a, l, r, start=s, stop=p), reads=[wu] + xT_rd(o, n), writes=[ps])
                ug = ugr.next()
                act(ug.ap[:, 0:n], ps.ap[:, 0:n], AF.Gelu_apprx_tanh, [ps], [ug])
                tt('dve', aT_ap[:, c, o:o + n], ug.ap[:, 0:n], zT_aps[c // 4][:, c % 4, o:o + n], ALU.mult,
                   [ug] + tl(zTh[c // 4], o, n), [aT])
        for r_ in r_zTs:
            A.drop(r_)
        A.drop(r_aw)
        A.drop(r_at)
        for r_ in r_wus + r_ugs:
            A.drop(r_)

        if _stop == "A":
            P.emit(block, semctx)
            return nc
        r_cT = A.take(8 * NTOK // 2, "cT")
        cT_ap = r_cT.bf(0, 8 * NTOK // 2).rearrange("p (k n) -> p k n", k=8)
        cT = [r_cT.buf(cT_ap[:, :, t * 128:(t + 1) * 128], "cT%d" % t) for t in range(NT)]
        r_qm = A.take(8 * NTOK // 2, "qmT")
        qm_ap = r_qm.bf(0, 8 * NTOK // 2).rearrange("p (k n) -> p k n", k=8)
        qmT = r_qm.buf(qm_ap, "qmT")
        r_cw = A.take(2048 + 4096 + 2048, "cwork")
        r_ckv = A.take(2048, "ckv")
        memT = r_cw.buf(r_cw.bf(0, 2048).rearrange("p (k m) -> p k m", k=16), "memT")
        wkr = Ring([wk0, r_cw.buf(r_cw.bf(2048, 6144).rearrange("p (k n) -> p k n", k=16), "wkp1")])
        mstage = Ring([r_cw.buf(r_cw.f32(6144 + i * 1024, 6144 + (i + 1) * 1024).rearrange("p (t n) -> p t n", t=2),
                                "mst%d" % i) for i in range(2)])
        kTp = r_ckv.buf(r_ckv.bf(0, 1024).rearrange("p (k m) -> p k m", k=8), "kTp")
        vp = r_ckv.buf(r_ckv.bf(1024, 2048).rearrange("p (t n) -> p t n", t=2), "vp")
        for mt in range(2):
            for hf in range(4):
                ps = psring.next()
                pv = psb(ps)
                for k in range(4):
                    kk = hf * 4 + k
                    trn(ps, pv[:, k * 128:(k + 1) * 128], memin, memin.ap[:, mt, kk * 128:(kk + 1) * 128])
                evac(memT.ap[:, hf * 4:(hf + 1) * 4, mt * 128:(mt + 1) * 128], pv.rearrange("p (k n) -> p k n", k=4),
                     [ps], [memT])
        if _stop == "C0a":
            P.emit(block, semctx)
            return nc
        for which, (wsrc, dsto) in enumerate(((w_mk, mk_o), (w_mv, mv_o))):
            for n in range(2):
                wk = wkr.next()
                if (which, n) != (0, 0):
                    load('pool', wk, wk.ap, wcols(wsrc, n * 512, 512))
                ms = mstage.next()
                for mt in range(2):
                    ps = psring.next()
                    for k in range(16):
                        mm(ps, ps.ap, memT, memT.ap[:, k, mt * 128:(mt + 1) * 128], wk, wk.ap[:, k, :], k == 0, k == 15)
                    evac(ms.ap[:, mt, :], ps.ap, [ps], [ms])
                    if which == 1:
                        evac(vp.ap[:, mt, n * 512:(n + 1) * 512], ps.ap, [ps], [vp])
                store('sp', dsto[:, n * 512:(n + 1) * 512].rearrange("(t p) n -> p t n", p=128), ms, ms.ap)
                if which == 0:
                    for cc in range(4):
                        ps = psring.next()
                        for k in range(16):
                            mm(ps, ps.ap[:, 0:256], wk, wk.ap[:, k, cc * 128:(cc + 1) * 128], memT, memT.ap[:, k, :], k == 0, k == 15)
                        evac(kTp.ap[:, n * 4 + cc, :], ps.ap[:, 0:256], [ps], [kTp])
        if _stop == "C1":
            P.emit(block, semctx)
            return nc
        A.drop(r_cw)
        A.drop(r_memin)
        A.drop(r_wk0)
        r_wq = A.take(3 * 1024, "wq")
        wqr = Ring([r_wq.buf(r_wq.bf(i * 1024, (i + 1) * 1024).rearrange("p (k n) -> p k n", k=16), "wq%d" % i)
                    for i in range(3)])
        for c in range(8):
            wq = wqr.next()
            load('pool', wq, wq.ap, wcols(w_in, OFF_MQ + c * 128, 128))
            for (o, n) in TG:
                ps = psring.next()
                for k in range(16):
                    P.op('pe', lambda e, a=ps.ap[:, 0:n], l=wq.ap[:, k, :], r=xT_ap[:, k, o:o + n], s=(k == 0), p=(k == 15):
                         e.matmul(a, l, r, start=s, stop=p), reads=[wq] + xT_rd(o, n), writes=[ps])
                evac(qm_ap[:, c, o:o + n], ps.ap[:, 0:n], [ps], [qmT])
        if _stop == "C2":
            P.emit(block, semctx)
            return nc
        A.drop(r_wq)
        r_sk = A.take(8 * 1024 + 2 * 1024 + 2 * 512, "skv")
        kin = Ring([r_sk.buf(r_sk.bf(i * 1024, (i + 1) * 1024).rearrange("p (t f) -> p t f", t=2), "kin%d" % i)
                    for i in range(6)])
        kTb = Ring([r_sk.buf(r_sk.bf(8192 + i * 1024, 8192 + (i + 1) * 1024).rearrange("p (k m) -> p k m", k=8), "kTb%d" % i)
                    for i in range(2)])
        qmb = Ring([r_sk.buf(r_sk.bf(10240 + i * 512, 10240 + (i + 1) * 512).rearrange("p (k n) -> p k n", k=8), "qmb%d" % i)
                    for i in range(2)])
        vin = Ring([r_sk.buf(r_sk.bf((6 + i) * 1024, (7 + i) * 1024).rearrange("p (t f) -> p t f", t=2), "vin%d" % i)
                    for i in range(2)])
        KPRE, VPRE = 6, 2
        kpre = []
        for b in range(KPRE):
            ki = kin.next()
            load('pool', ki, ki.ap, ck[b].rearrange("(t p) f -> p t f", p=128))
            kpre.append(ki)
        vpre = []
        for b in range(VPRE):
            vi = vin.next()
            load('pool', vi, vi.ap, cv[b].rearrange("(t p) f -> p t f", p=128))
            vpre.append(vi)
        SC = 1.0 / 16.0

        def softmax_p(ps_s, hh):
            st = sst.ap[:, hh, :]
            P.op('dve', lambda e: e.reduce_max(st[:, 0:1], ps_s.ap[:, 0:256], AX.X), reads=[ps_s], writes=[sst])
            ts('dve', st[:, 1:2], st[:, 0:1], -SC, None, ALU.mult, None, [sst], [sst])
            pe_ = pex.next()
            P.op('dve', lambda e: e.memset(st[:, 2:3], 0.0), reads=[], writes=[sst])
            act(pe_.ap, ps_s.ap[:, 0:256], AF.Exp, [ps_s, sst], [pe_, sst], bias=st[:, 1:2], scale=SC, accum=st[:, 2:3])
            P.op('dve', lambda e: e.reciprocal(st[:, 3:4], st[:, 2:3]), reads=[sst], writes=[sst])
            pb = pbf.next()
            ts('dve', pb.ap, pe_.ap, st[:, 3:4], None, ALU.mult, None, [pe_, sst], [pb])
            return pb

        def transpose_p(pb, pT):
            psT = psring.next()
            pv = psb(psT)
            trn(psT, pv[:, 0:128], pb, pb.ap[:, 0:128])
            trn(psT, pv[:, 128:256], pb, pb.ap[:, 128:256])
            evac(pT.ap, pv[:, 0:256].rearrange("p (t n) -> p t n", t=2), [psT], [pT])

        def softmax_to_pT(ps_s, hh, pT):
            transpose_p(softmax_p(ps_s, hh), pT)

        r_pa = ([A.take(1024, "pe4_%d" % i) for i in range(2)] + [A.take(512, "pb4_%d" % i) for i in range(2)]
                + [A.take(512, "pT4_%d" % i) for i in range(2)] + [A.take(32, "st4")])
        pe4 = Ring([r_pa[i].buf(r_pa[i].f32(0, 1024).rearrange("p (h m) -> p h m", h=4), "pe4_%d" % i) for i in range(2)])
        pb4 = Ring([r_pa[2 + i].buf(r_pa[2 + i].bf(0, 512).rearrange("p (h m) -> p h m", h=4), "pb4_%d" % i) for i in range(2)])
        pT4 = Ring([r_pa[4 + i].buf(r_pa[4 + i].bf(0, 512).rearrange("p (h t n) -> p h t n", h=4, t=2), "pT4_%d" % i)
                    for i in range(2)])
        st4 = Ring([(r_pa[6].buf(r_pa[6].f32(16 * i, 16 * i + 8), "st4a%d" % i),
                     r_pa[6].buf(r_pa[6].f32(16 * i + 8, 16 * i + 16), "st4b%d" % i)) for i in range(2)])
        cctx = {}

        def c_s0(t):
            pss = [psring.next(), psring.next()]
            for hh in range(4):
                ps = pss[hh // 2]
                off = (hh % 2) * 256
                for j in range(2):
                    P.op('pe', lambda e, a=ps.ap[:, off:off + 256], l=qm_ap[:, 2 * hh + j, t * 128:(t + 1) * 128],
                         r=kTp.ap[:, 2 * hh + j, :], s=(j == 0), p=(j == 1): e.matmul(a, l, r, start=s, stop=p),
                         reads=[qmT, kTp], writes=[ps])
            sa_, sb__ = st4.next()
            for pr in range(2):
                P.op('dve', lambda e, o=sa_.ap[:, 2 * pr:2 * pr + 2], i=pss[pr].ap.rearrange("p (h m) -> p h m", h=2):
                     e.reduce_max(o, i, AX.X), reads=[pss[pr]], writes=[sa_])
            ts('dve', sa_.ap[:, 4:8], sa_.ap[:, 0:4], -SC, None, ALU.mult, None, [sa_], [sa_])
            P.op('dve', lambda e: e.memset(sb__.ap[:, 0:4], 0.0), reads=[], writes=[sb__])
            pe_ = pe4.next()
            for hh in range(4):
                ps = pss[hh // 2]
                off = (hh % 2) * 256
                act(pe_.ap[:, hh, :], ps.ap[:, off:off + 256], AF.Exp, [ps, sa_, sb__], [pe_, sb__],
                    bias=sa_.ap[:, 4 + hh:5 + hh], scale=SC, accum=sb__.ap[:, hh:hh + 1])
            P.op('dve', lambda e: e.reciprocal(sb__.ap[:, 4:8], sb__.ap[:, 0:4]), reads=[sb__], writes=[sb__])
            pb = pb4.next()
            tt('dve', pb.ap, pe_.ap, sb__.ap[:, 4:8].unsqueeze(2).to_broadcast([128, 4, 256]), ALU.mult, [pe_, sb__], [pb])
            cctx[t] = pb

        def c_s1(t):
            pb = cctx[t]
            pT = pT4.next()
            for pr in range(2):
                ps = psring.next()
                for q_ in range(4):
                    hh = 2 * pr + q_ // 2
                    mt = q_ % 2
                    trn(ps, ps.ap[:, q_ * 128:(q_ + 1) * 128], pb, pb.ap[:, hh, mt * 128:(mt + 1) * 128])
                evac(pT.ap[:, 2 * pr:2 * pr + 2, :, :], ps.ap.rearrange("p (h t n) -> p h t n", h=2, t=2), [ps], [pT])
            cctx[t] = pT

        def c_s2(t):
            pT = cctx.pop(t)
            for pr in range(2):
                ps_c = psring.next()
                for q_ in range(4):
                    ec = 4 * pr + q_
                    hh = ec // 2
                    for mt in range(2):
                        mm(ps_c, ps_c.ap[:, q_ * 128:(q_ + 1) * 128], vp, vp.ap[:, mt, ec * 128:(ec + 1) * 128],
                           pT, pT.ap[:, hh, mt, :], mt == 0, mt == 1)
                evac(cT[t].ap[:, 4 * pr:4 * pr + 4, :], ps_c.ap.rearrange("p (k n) -> p k n", k=4), [ps_c], [cT[t]])

        run_pipeline(list(range(NOWN)), [(0, c_s0), (1, c_s1), (2, c_s2)])
        for r_ in r_pa:
            A.drop(r_)
        r_sw = A.take(4 * 8 + 4 * 256 + 4 * 128 + 4 * 128, "swork")
        sst = r_sw.buf(r_sw.f32(0, 32).rearrange("p (h s) -> p h s", h=4), "sst")
        pex = Ring([r_sw.buf(r_sw.f32(32 + i * 256, 32 + (i + 1) * 256), "pex%d" % i) for i in range(4)])
        pbf = Ring([r_sw.buf(r_sw.bf(1056 + i * 128, 1056 + (i + 1) * 128), "pbf%d" % i) for i in range(4)])
        pTr = [r_sw.buf(r_sw.bf(1568 + i * 128, 1568 + (i + 1) * 128).rearrange("p (t n) -> p t n", t=2), "pT%d" % i)
               for i in range(4)]
        if _stop == "C3":
            P.emit(block, semctx)
            return nc
        A.drop(r_ckv)
        NPRESL = 10
        r_mss = [A.take(512, "msl%d" % i) for i in range(NPRESL)]
        slot_bufs = [r_mss[i].buf(r_mss[i].bf(0, 512).rearrange("p (k n) -> p k n", k=8), "msl%d" % i) for i in range(NPRESL)]

        def m_srcs(mc):
            return [(w_pa, mc, 0), (w_pb, mc, 0), (w_pb, mc, 8), (w_pc, mc, 0),
                    (w_in, 0 * D + mc, 0), (w_in, 0 * D + mc, 8), (w_in, 1 * D + mc, 0), (w_in, 1 * D + mc, 8),
                    (w_in, 2 * D + mc, 0), (w_in, 2 * D + mc, 8)]

        for i_, (wsrc, c0, k0) in enumerate(m_srcs(0)):
            load('pool', slot_bufs[i_], slot_bufs[i_].ap, wcols(wsrc, c0, 128, k0, 8))
        so = NOWN * 128
        ps_acc = [PS[i] for i in range(4)]
        tring = Ring([PS[i] for i in range(4, 8)])
        for b in range(16):
            if b < KPRE:
                ki = kpre[b]
            else:
                ki = kin.next()
                load('pool', ki, ki.ap, ck[b].rearrange("(t p) f -> p t f", p=128))
            kt = kTb.next()
            for mt in range(2):
                for q_ in range(2):
                    ps = tring.next()
                    pv = psb(ps)
                    for c4 in range(4):
                        c = q_ * 4 + c4
                        trn(ps, pv[:, c4 * 128:(c4 + 1) * 128], ki, ki.ap[:, mt, c * 128:(c + 1) * 128])
                    evac(kt.ap[:, q_ * 4:(q_ + 1) * 4, mt * 128:(mt + 1) * 128], pv.rearrange("p (k n) -> p k n", k=4), [ps], [kt])
            qb_ = qmb.next()
            tt('dve', qb_.ap, qm_ap[:, :, so:so + 128], blockmask[:, b:b + 1, :].to_broadcast([128, 8, 128]), ALU.mult,
               [qmT, CB], [qb_])
            for hh in range(4):
                for j in range(2):
                    mm(ps_acc[hh], ps_acc[hh].ap[:, 0:256], qb_, qb_.ap[:, 2 * hh + j, :], kt, kt.ap[:, 2 * hh + j, :],
                       b == 0 and j == 0, b == 15 and j == 1)
        if _stop == "C4":
            P.emit(block, semctx)
            return nc
        psring_saved = psring
        psring = tring
        for hh in range(4):
            softmax_to_pT(ps_acc[hh], hh, pTr[hh])
        if _stop == "C5":
            P.emit(block, semctx)
            return nc
        vring2 = Ring(list(vin.items) + list(kin.items))
        vring2.i = VPRE
        acc_c = [PS[0], PS[1]]
        for b in range(16):
            if b < VPRE:
                vi = vpre[b]
            else:
                vi = vring2.next()
                load('pool', vi, vi.ap, cv[b].rearrange("(t p) f -> p t f", p=128))
            for ec in range(8):
                pc = acc_c[ec // 4]
                for mt in range(2):
                    mm(pc, pc.ap[:, (ec % 4) * 128 + 8 * b:(ec % 4) * 128 + 8 * b + 8], vi, vi.ap[:, mt, ec * 128:(ec + 1) * 128],
                       pTr[ec // 2], pTr[ec // 2].ap[:, mt, 8 * b:8 * b + 8], mt == 0, mt == 1)
        for i in range(2):
            evac(cT[NOWN].ap[:, i * 4:(i + 1) * 4, :], acc_c[i].ap.rearrange("p (k n) -> p k n", k=4), [acc_c[i]], [cT[NOWN]])
        psring = psring_saved
        A.drop(r_sk)
        A.drop(r_sw)
        A.drop(r_qm)

        if _stop == "DBG":
            r_d = A.take(4096, "dbg")
            d1 = r_d.buf(r_d.f32(0, 2048), "d1")
            d2 = r_d.buf(r_d.f32(2048, 4096), "d2")
            so_ = NOWN * 128
            P.op('dve', lambda e: e.tensor_copy(d1.ap.rearrange("p (k n) -> p k n", k=16), bT_ap[:, :, so_:so_ + 128]), reads=[bT[NOWN]], writes=[d1])
            store('sp', y_smp[:, :], d1, d1.ap)
            P.op('dve', lambda e: e.tensor_copy(d2.ap[:, 0:1024].rearrange("p (k n) -> p k n", k=8), aT_ap[:, :, so_:so_ + 128]), reads=[aT], writes=[d2])
            P.op('dve', lambda e: e.tensor_copy(d2.ap[:, 1024:2048].rearrange("p (k n) -> p k n", k=8), cT_ap[:, :, so_:so_ + 128]), reads=[cT[NOWN]], writes=[d2])
            store('sp', y_own[0:128, :], d2, d2.ap)
            P.emit(block, semctx)
            return nc
        if _stop == "C":
            P.emit(block, semctx)
            return nc
        r_wo0 = A.take(4096, "wo0")
        WO0 = r_wo0.buf(r_wo0.bf(0, 4096).rearrange("p (k n) -> p k n", k=16), "wo0")
        r_mT = A.take(16 * NTOK // 2, "mT")
        mT_ap = r_mT.bf(0, 16 * NTOK // 2).rearrange("p (k n) -> p k n", k=16)
        mT = [r_mT.buf(mT_ap[:, :, t * 128:(t + 1) * 128], "mT%d" % t) for t in range(NT)]
        NSLOT = 14
        for i in range(NPRESL, NSLOT):
            r_mss.append(A.take(512, "msl%d" % i))
            slot_bufs.append(r_mss[i].buf(r_mss[i].bf(0, 512).rearrange("p (k n) -> p k n", k=8), "msl%d" % i))
        slots = Ring(slot_bufs)
        slots.i = NPRESL
        r_mws = [A.take(512, "mw%d" % i) for i in range(5)]
        sgm = [r_mws[i].buf(r_mws[i].f32(0, 512), "sgm%d" % i) for i in range(3)]
        acc_m = [r_mws[3 + i].buf(r_mws[3 + i].f32(0, 512), "accm%d" % i) for i in range(2)]
        for m in range(16):
            mc = m * 128
            if m == 11:
                load('pool', WO0, WO0.ap, wcols(w_out, 0, 512))
            sl = []
            if m == 0:
                sl = list(slot_bufs[0:NPRESL])
            else:
                for (wsrc, c0, k0) in m_srcs(mc):
                    s_ = slots.next()
                    load('pool', s_, s_.ap, wcols(wsrc, c0, 128, k0, 8))
                    sl.append(s_)
            for (o, n) in TG:
                def group(ps, wl, src_ap, src_bufs):
                    nk = len(wl) * 8
                    for k in range(nk):
                        w_ = wl[k // 8]
                        P.op('pe', lambda e, a=ps.ap[:, 0:n], l=w_.ap[:, k % 8, :], r=src_ap[:, k, o:o + n], s=(k == 0), p=(k == nk - 1):
                             e.matmul(a, l, r, start=s, stop=p), reads=[w_] + src_bufs, writes=[ps])
                ps_a, ps_b, ps_c = psring.next(), psring.next(), psring.next()
                ps_g = [psring.next(), psring.next(), psring.next()]
                group(ps_g[0], sl[4:6], xT_ap, xT_rd(o, n))
                group(ps_a, sl[0:1], aT_ap, [aT])
                group(ps_g[1], sl[6:8], xT_ap, xT_rd(o, n))
                group(ps_b, sl[1:3], bT_ap, tl(bT, o, n))
                group(ps_g[2], sl[8:10], xT_ap, xT_rd(o, n))
                group(ps_c, sl[3:4], cT_ap, tl(cT, o, n))
                for br in range(3):
                    act(sgm[br].ap[:, 0:n], ps_g[br].ap[:, 0:n], AF.Sigmoid, [ps_g[br]], [sgm[br]])
                tt('dve', acc_m[0].ap[:, 0:n], sgm[0].ap[:, 0:n], ps_a.ap[:, 0:n], ALU.mult, [sgm[0], ps_a], [acc_m[0]])
                tt('dve', acc_m[1].ap[:, 0:n], sgm[1].ap[:, 0:n], ps_b.ap[:, 0:n], ALU.mult, [sgm[1], ps_b], [acc_m[1]])
                tt('dve', acc_m[0].ap[:, 0:n], acc_m[0].ap[:, 0:n], acc_m[1].ap[:, 0:n], ALU.add, [acc_m[0], acc_m[1]], [acc_m[0]])
                tt('dve', acc_m[1].ap[:, 0:n], sgm[2].ap[:, 0:n], ps_c.ap[:, 0:n], ALU.mult, [sgm[2], ps_c], [acc_m[1]])
                tt('dve', mT_ap[:, m, o:o + n], acc_m[0].ap[:, 0:n], acc_m[1].ap[:, 0:n], ALU.add, [acc_m[0], acc_m[1]], tl(mT, o, n))
        for r_ in r_mws + r_mss:
            A.drop(r_)
        A.drop(r_xT)
        A.drop(r_aT)
        A.drop(r_bT)
        A.drop(r_cT)

        if _stop == "M":
            P.emit(block, semctx)
            return nc
        r_r = A.take(NT * 2048, "resid")
        R = [r_r.buf(r_r.f32(t * 2048, (t + 1) * 2048), "r%d" % t) for t in range(NT)]
        r_ln = A.take(2 * 2048, "lntab")
        LNT = r_ln.buf(r_ln.f32(0, 4096), "lnt")
        load('sp', LNT, LNT.ap, vec_ln[0:1, 0:4096].partition_broadcast(128))
        r_x1b = A.take(2 * 1024, "x1b")
        x1b = Ring([r_x1b.buf(r_x1b.bf(i * 1024, (i + 1) * 1024), "x1b%d" % i) for i in range(2)])
        r_wos = [r_wo0] + [A.take(4096, "wo%d" % i) for i in range(1, 4)]
        WO = [WO0] + [r_wos[i].buf(r_wos[i].bf(0, 4096).rearrange("p (k n) -> p k n", k=16), "wo%d" % i) for i in range(1, 4)]
        for n in range(1, 4):
            load('pool', WO[n], WO[n].ap, wcols(w_out, n * 512, 512))
        r_xrs = [A.take(512, "xres%d" % i) for i in range(3)]
        xrr = Ring([r_xrs[i].buf(r_xrs[i].f32(0, 512), "xres%d" % i) for i in range(3)])
        r_lw = A.take(64, "lnwork")
        lstr = Ring([r_lw.buf(r_lw.f32(i * 32, i * 32 + 32), "lst%d" % i) for i in range(2)])

        def layer_norm_a(rt):
            ls = lstr.next()
            st6 = ls.ap[:, 0:24].rearrange("p (c s) -> p c s", c=4)
            for c in range(4):
                P.op('dve', lambda e, a=st6[:, c, :], b=rt.ap[:, c * 512:(c + 1) * 512]: e.bn_stats(a, b), reads=[rt], writes=[ls])
            P.op('dve', lambda e: e.bn_aggr(ls.ap[:, 24:26], ls.ap[:, 0:24]), reads=[ls], writes=[ls])
            act(ls.ap[:, 27:28], ls.ap[:, 25:26], AF.Sqrt, [ls, EPSB], [ls], bias=EPSB.ap[:, 0:1])
            P.op('dve', lambda e: e.reciprocal(ls.ap[:, 28:29], ls.ap[:, 27:28]), reads=[ls], writes=[ls])
            stt(ls.ap[:, 29:30], ls.ap[:, 24:25], -1.0, ls.ap[:, 28:29], ALU.mult, ALU.mult, [ls], [ls])
            act(rt.ap, rt.ap, AF.Identity, [rt, ls], [rt], bias=ls.ap[:, 29:30], scale=ls.ap[:, 28:29])

        def layer_norm_b(rt, out_bf=None, final_scale=None):
            tt('dve', rt.ap, rt.ap, LNT.ap[:, 0:2048], ALU.mult, [rt, LNT], [rt])
            tt('pool', rt.ap, rt.ap, LNT.ap[:, 2048:4096], ALU.add, [rt, LNT], [rt])
            if out_bf is not None:
                evac(out_bf.ap, rt.ap, [rt], [out_bf], eng='act')
                act(rt.ap, rt.ap, AF.Copy, [rt], [rt], scale=final_scale)

        x1T = mT
        octx = {}

        def o_s0(t):
            for n in range(4):
                xr = xrr.next()
                load('sp', xr, xr.ap, xrows(t)[:, n * 512:(n + 1) * 512])
                ps = psring.next()
                for k in range(16):
                    mm(ps, ps.ap, mT[t], mT[t].ap[:, k, :], WO[n], WO[n].ap[:, k, :], k == 0, k == 15)
                stt(R[t].ap[:, n * 512:(n + 1) * 512], xr.ap, ALPHA, ps.ap, ALU.mult, ALU.add, [xr, ps], [R[t]])
            layer_norm_a(R[t])

        def o_s05(t):
            xb_ = x1b.next()
            layer_norm_b(R[t], out_bf=xb_, final_scale=ALPHA)
            octx[t] = xb_

        def o_s1(t):
            xb_ = octx.pop(t)
            for hf in range(4):
                ps = psring.next()
                pv = psb(ps)
                for k in range(4):
                    kk = hf * 4 + k
                    trn(ps, pv[:, k * 128:(k + 1) * 128], xb_, xb_.ap[:, kk * 128:(kk + 1) * 128])
                evac(x1T[t].ap[:, hf * 4:(hf + 1) * 4, :], pv.rearrange("p (k n) -> p k n", k=4), [ps], [x1T[t]])

        run_pipeline(list(range(NT)), [(0, o_s0), (1, o_s05), (2, o_s1)])
        for r_ in r_wos:
            A.drop(r_)
        for r_ in r_xrs:
            A.drop(r_)
        A.drop(r_x1b)
        if _stop == "O":
            P.emit(block, semctx)
            return nc

        GC = 4
        NG = 44 // GC
        r_hT = A.take(2 * GC * NTOK // 2, "hT")
        hTr = Ring([r_hT.buf(r_hT.bf(i * GC * NTOK // 2, (i + 1) * GC * NTOK // 2).rearrange("p (k n) -> p k n", k=GC), "hT%d" % i)
                    for i in range(2)])
        r_wg = A.take(6 * 1024, "wgu")
        wgr = Ring([r_wg.buf(r_wg.bf(i * 1024, (i + 1) * 1024).rearrange("p (k n) -> p k n", k=16), "wgu%d" % i)
                    for i in range(6)])
        r_wd = A.take(4 * GC * 256, "wd")
        wdr = Ring([r_wd.buf(r_wd.bf(i * GC * 256, (i + 1) * GC * 256).rearrange("p (k n) -> p k n", k=GC), "wd%d" % i)
                    for i in range(4)])
        r_sg = A.take(3 * 512, "sgt")
        sgr = Ring([r_sg.buf(r_sg.f32(i * 512, (i + 1) * 512), "sgt%d" % i) for i in range(3)])
        for g in range(NG):
            hT = hTr.next()
            for cl in range(GC):
                c = g * GC + cl
                wg = wgr.next()
                wu_ = wgr.next()
                load('pool', wg, wg.ap, wcols(w_fg, c * 128, 128))
                load('pool', wu_, wu_.ap, wcols(w_fu, c * 128, 128))
                for (o, n) in TG:
                    ps_g, ps_u = psring.next(), psring.next()
                    for (ps, w_) in ((ps_g, wg), (ps_u, wu_)):
                        for k in range(16):
                            P.op('pe', lambda e, a=ps.ap[:, 0:n], l=w_.ap[:, k, :], r=mT_ap[:, k, o:o + n], s=(k == 0), p=(k == 15):
                                 e.matmul(a, l, r, start=s, stop=p), reads=[w_] + tl(x1T, o, n), writes=[ps])
                    sg = sgr.next()
                    act(sg.ap[:, 0:n], ps_g.ap[:, 0:n], AF.Silu, [ps_g], [sg])
                    tt('dve', hT.ap[:, cl, o:o + n], sg.ap[:, 0:n], ps_u.ap[:, 0:n], ALU.mult, [sg, ps_u], [hT])
            wds = []
            for n in range(4):
                wd = wdr.next()
                load('pool', wd, wd.ap, wcols(w_fd, n * 512, 512, g * GC, GC))
                wds.append(wd)
            if g == NG - 1:
                load('sp', LNT, LNT.ap, vec_ln[0:1, 4096:8192].partition_broadcast(128))
            if g < NG - 1:
                for n in range(4):
                    for t in range(NT):
                        ps = psring.next()
                        for k in range(GC):
                            mm(ps, ps.ap, hT, hT.ap[:, k, t * 128:(t + 1) * 128], wds[n], wds[n].ap[:, k, :], k == 0, k == GC - 1)
                        tt('dve', R[t].ap[:, n * 512:(n + 1) * 512], R[t].ap[:, n * 512:(n + 1) * 512], ps.ap, ALU.add, [R[t], ps], [R[t]])
            else:
                def f_s0(t, hT=hT, wds=wds):
                    for n in range(4):
                        ps = psring.next()
                        for k in range(GC):
                            mm(ps, ps.ap, hT, hT.ap[:, k, t * 128:(t + 1) * 128], wds[n], wds[n].ap[:, k, :], k == 0, k == GC - 1)
                        tt('dve', R[t].ap[:, n * 512:(n + 1) * 512], R[t].ap[:, n * 512:(n + 1) * 512], ps.ap, ALU.add, [R[t], ps], [R[t]])
                    layer_norm_a(R[t])

                def f_s1(t):
                    layer_norm_b(R[t])
                    store('sp', yrows(t), R[t], R[t].ap)

                run_pipeline(list(range(NT)), [(0, f_s0), (1, f_s1)])
        P.emit(block, semctx)
    return nc


def _core_consts(half):
    f32 = np.float32
    hd = 64
    inv = 10000.0 ** (-(np.arange(hd, dtype=np.float64) / float(hd)))
    p = np.arange(128)
    cos2 = np.zeros((128, 17, 128), f32)
    sin2 = np.zeros((128, 17, 128), f32)
    for ti in range(17):
        if ti < 8:
            pos = (ti * 128 + p).astype(np.float64)
        elif ti < 16:
            pos = (half * 1024 + (ti - 8) * 128 + p).astype(np.float64)
        else:
            pos = (16384 + (p % 8)).astype(np.float64)
        ang = pos[:, None] * inv[None, :]
        c = np.cos(ang).astype(f32)
        s = np.sin(ang).astype(f32)
        cos2[:, ti, :64] = c
        cos2[:, ti, 64:] = c
        sin2[:, ti, :64] = -s
        sin2[:, ti, 64:] = s
    hh = np.arange(8, dtype=np.float64)
    lg = np.log1p(-(2.0 ** (-5.0 - hh)))
    qs = np.zeros((128, 16), f32)
    ks = np.zeros((128, 16), f32)
    ip = p.astype(np.float64)
    isq = (p % 8).astype(np.float64)
    qs[:, 0:8] = np.exp((ip[:, None] + 1.0) * lg[None, :])
    ks[:, 0:8] = np.exp(-(ip[:, None] + 1.0) * lg[None, :]) * (128.0 ** -0.5)
    qs[:, 8:16] = np.exp((isq[:, None] + 1.0) * lg[None, :])
    ks[:, 8:16] = np.exp(-(isq[:, None] + 1.0) * lg[None, :]) * (128.0 ** -0.5)
    rowmask = (p[:, None] // 8 == np.arange(16)[None, :]).astype(f32)
    cst_f = np.concatenate([cos2.reshape(128, -1), sin2.reshape(128, -1), qs, ks, rowmask], axis=1).astype(f32)
    ident = np.eye(128, dtype=f32)
    j = p[:, None]
    i = p[None, :]
    maskT_p = (i >= j).astype(f32)
    maskT_s = ((i >= j) & (i // 8 == j // 8)).astype(f32)
    bm = (np.arange(128)[None, :] // 8 == np.arange(16)[:, None]).astype(f32)
    blockmask = np.broadcast_to(bm.reshape(1, 16 * 128), (128, 16 * 128))
    cst_b = np.concatenate([ident, maskT_p, maskT_s, blockmask], axis=1).astype(f32)
    return np.ascontiguousarray(cst_f), np.ascontiguousarray(cst_b)


_NC_CACHE = {}


def kernel(x_prompt, x_sample, mem_prompt, state_ret, cache_mem_k, cache_mem_v,
           w_in, sgu_ln_g, sgu_ln_b, sgu_w, sgu_b, w_proj_a, ret_gn_g, w_proj_b,
           w_mem_k, w_mem_v, w_proj_c, w_out, ln1_g, ln1_b,
           w_ffn_gate, w_ffn_up, w_ffn_down, ln2_g, ln2_b):
    f32 = np.float32
    A_ = lambda a: np.ascontiguousarray(np.asarray(a, dtype=f32))
    x_prompt, x_sample, mem_prompt = A_(x_prompt), A_(x_sample), A_(mem_prompt)
    state_ret, cache_mem_k, cache_mem_v = A_(state_ret), A_(cache_mem_k), A_(cache_mem_v)
    shared = {
        "w_in": A_(w_in)[0], "w_pa": A_(w_proj_a)[0], "w_pb": A_(w_proj_b)[0], "w_pc": A_(w_proj_c)[0],
        "w_mk": A_(w_mem_k)[0], "w_mv": A_(w_mem_v)[0], "w_out": A_(w_out)[0],
        "w_fg": A_(w_ffn_gate)[0], "w_fu": A_(w_ffn_up)[0], "w_fd": A_(w_ffn_down)[0],
    }
    sw = A_(sgu_w)[0]
    wT_p = sw.transpose(2, 0, 1)
    wT_s = np.zeros((128, 4, 128), f32)
    for b in range(16):
        wT_s[8 * b:8 * b + 8, :, 8 * b:8 * b + 8] = sw[:, :8, :8].transpose(2, 0, 1)
    shared["sgu_wT"] = np.ascontiguousarray(np.stack([wT_p, wT_s], axis=1))
    sb = A_(sgu_b)[0]
    sb_s = np.tile(sb[:, :8], (1, 16))
    shared["sgu_bb"] = np.ascontiguousarray(np.stack([sb, sb_s], axis=0).reshape(1, -1))
    shared["vec_a"] = np.ascontiguousarray(np.concatenate([A_(sgu_ln_g)[0].reshape(-1), A_(sgu_ln_b)[0].reshape(-1)]).reshape(1, -1))
    shared["vec_g"] = np.ascontiguousarray(A_(ret_gn_g)[0].reshape(1, -1))
    shared["vec_ln"] = np.ascontiguousarray(np.concatenate([A_(ln1_g)[0], A_(ln1_b)[0], A_(ln2_g)[0], A_(ln2_b)[0]]).reshape(1, -1))

    in_maps = []
    for c in range(8):
        b, half = c // 2, c % 2
        cf, cb = _core_consts(half)
        m = dict(shared)
        m["x_own"] = np.ascontiguousarray(x_prompt[b, half * 1024:(half + 1) * 1024])
        m["x_pre"] = np.ascontiguousarray(x_prompt[b, 0:1024]) if half == 1 else np.zeros((1024, D), f32)
        m["x_smp"] = np.ascontiguousarray(x_sample[16 * c:16 * c + 16].reshape(128, D))
        m["mem"] = np.ascontiguousarray(mem_prompt[b])
        m["state"] = np.ascontiguousarray(state_ret[0, 16 * c:16 * c + 16])
        m["ck"] = np.ascontiguousarray(cache_mem_k[0, 16 * c:16 * c + 16].reshape(16, 256, 1024))
        m["cv"] = np.ascontiguousarray(cache_mem_v[0, 16 * c:16 * c + 16].reshape(16, 256, 1024))
        m["cst_f"] = cf
        m["cst_b"] = cb
        in_maps.append(m)

    if _os.environ.get("MK_ONECORE"):
        return in_maps
    if "nc" not in _NC_CACHE:
        _NC_CACHE["nc"] = build_program()
    nc = _NC_CACHE["nc"]
    res = run_bass_kernel_spmd(nc, in_maps, core_ids=list(range(8)))
    R = res.results
    y_prompt = np.zeros((4, 2048, D), f32)
    y_sample = np.zeros((128, 8, D), f32)
    st_p = np.zeros((1, 4, 8, 128, 256), f32)
    mk = np.zeros((1, 4, 256, 4, 256), f32)
    mv = np.zeros((1, 4, 256, 4, 256), f32)
    st_s = np.zeros((1, 128, 8, 128, 256), f32)
    cvs = np.zeros((1, 128, 8, 4, 256), f32)
    for c in range(8):
        b, half = c // 2, c % 2
        r = R[c]
        y_prompt[b, half * 1024:(half + 1) * 1024] = r["y_own"]
        y_sample[16 * c:16 * c + 16] = r["y_smp"].reshape(16, 8, D)
        st_s[0, 16 * c:16 * c + 16] = r["st_s"]
        cvs[0, 16 * c:16 * c + 16] = r["cv_s"].reshape(16, 8, 4, 256)
        if half == 1:
            st_p[0, b] = r["st_p"]
        else:
            mk[0, b] = r["mk_o"].reshape(256, 4, 256)
            mv[0, b] = r["mv_o"].reshape(256, 4, 256)
    return (y_prompt, y_sample, st_p, mk, mv, st_s, cvs)
```

```python
import contextlib
import os as _os
import math
import numpy as np
import concourse.bass as bass
import concourse.mybir as mybir
from concourse.bass_utils import run_bass_kernel_spmd

F32 = mybir.dt.float32
BF16 = mybir.dt.bfloat16
AF = mybir.ActivationFunctionType
ALU = mybir.AluOpType
AX = mybir.AxisListType

D = 2048
NPRE = 8
NOWN = 8
NT = NOWN + 1
NTOK = NT * 128
A_W = 1024
OFF_AU = 3 * D
OFF_AV = OFF_AU + 1024
OFF_RQ = OFF_AV + 1024
OFF_RK = OFF_RQ + 1024
OFF_RV = OFF_RK + 1024
OFF_RG = OFF_RV + 2048
OFF_MQ = OFF_RG + 2048
IN_W = OFF_MQ + 1024
D_FF = 5632
ALPHA = 2.0 ** 0.25
EPS = 1e-5
ARENA_WORDS = 53200
TG = [(0, 512), (512, 512), (1024, 128)]

COMPUTE = ('pe', 'act', 'dve', 'pool')
NOSELF = tuple(_os.environ.get('MK_NOSELF', 'pe').split(','))


class Inst:
    __slots__ = ('eng', 'fn', 'deps', 'need_inc', 'semval', 'is_dma', 'anchor', 'dma_val')

    def __init__(self, eng, fn, is_dma=False):
        self.eng = eng
        self.fn = fn
        self.deps = []
        self.need_inc = False
        self.semval = None
        self.is_dma = is_dma
        self.anchor = None
        self.dma_val = None


class Buf:
    def __init__(self, ap, name='', ghost=None):
        self.ap = ap
        self.name = name
        self.last_w = None
        self.readers = {}
        self.dma_readers = []
        self.ghost = list(ghost) if ghost else []
        self.dma_count = 0
        self.sem = None
        self.exclusive = False


class Prog:
    def __init__(self, nc):
        self.nc = nc
        self.streams = {e: [] for e in ('pe', 'act', 'dve', 'pool', 'sp')}
        self.anchors = []

    def _track(self, X, reads, writes):
        deps = []
        for b in reads:
            if b.last_w is not None:
                deps.append(b.last_w)
            if b.ghost:
                deps.extend(b.ghost)
            if b.exclusive:
                deps.extend(r for en, r in b.readers.items() if en != X.eng)
        for b in writes:
            if b.last_w is not None:
                deps.append(b.last_w)
            deps.extend(b.readers.values())
            deps.extend(b.dma_readers)
            if b.ghost:
                deps.extend(b.ghost)
                b.ghost = []
        seen = set()
        for d in deps:
            if d is X or id(d) in seen:
                continue
            seen.add(id(d))
            X.deps.append(d)
        for b in reads:
            if any(b is w for w in writes):
                continue
            if X.is_dma:
                b.dma_readers.append(X)
            else:
                b.readers[X.eng] = X
        for b in writes:
            b.last_w = X
            b.readers = {}
            b.dma_readers = []

    def op(self, eng, fn, reads=(), writes=()):
        X = Inst(eng, fn)
        self._track(X, list(reads), list(writes))
        self.streams[eng].append(X)
        return X

    def dma(self, queue, fn, anchor, reads=(), writes=()):
        X = Inst(queue, fn, is_dma=True)
        X.anchor = anchor
        anchor.dma_count += 16
        X.dma_val = anchor.dma_count
        if not any(a is anchor for a in self.anchors):
            self.anchors.append(anchor)
        self._track(X, list(reads), list(writes))
        self.streams[queue].append(X)
        return X

    def retire(self, bufs):
        g = []
        for b in bufs:
            if b.last_w is not None:
                g.append(b.last_w)
            g.extend(b.readers.values())
            g.extend(b.dma_readers)
            g.extend(b.ghost)
        return g

    def emit(self, block, semctx):
        for e, lst in self.streams.items():
            for X in lst:
                for d in X.deps:
                    if d.is_dma:
                        continue
                    if d.eng == X.eng and (not X.is_dma) and d.eng in NOSELF:
                        continue
                    d.need_inc = True
        esem = {e: semctx("eng_" + e) for e in COMPUTE}
        for e, lst in self.streams.items():
            c = 0
            for X in lst:
                if (not X.is_dma) and X.need_inc:
                    c += 1
                    X.semval = c
        for i, a in enumerate(self.anchors):
            a.sem = semctx("dma_%d" % i)
        final_waits = [(a.sem, a.dma_count) for a in self.anchors]

        def run_stream(e, handle, final=False):
            waited = {}
            for X in self.streams[e]:
                for d in X.deps:
                    if d.is_dma:
                        s, v = d.anchor.sem, d.dma_val
                    else:
                        if d.eng == e and (not X.is_dma) and d.eng in NOSELF:
                            continue
                        s, v = esem[d.eng], d.semval
                    k = id(s)
                    if waited.get(k, 0) >= v:
                        continue
                    waited[k] = v
                    handle.wait_ge(s, v)
                ins = X.fn(handle)
                if X.is_dma:
                    ins.then_inc(X.anchor.sem, 16)
                elif X.need_inc:
                    ins.then_inc(esem[e], 1)
            if final:
                for s, v in final_waits:
                    handle.wait_ge(s, v)

        @block.tensor
        def _(t):
            run_stream('pe', t)

        @block.scalar
        def _(a):
            run_stream('act', a)

        @block.vector
        def _(v):
            run_stream('dve', v)

        @block.gpsimd
        def _(g):
            run_stream('pool', g)

        @block.sync
        def _(s):
            run_stream('sp', s, final=True)


class Region:
    def __init__(self, arena, off, n, ghosts):
        self.arena = arena
        self.off = off
        self.n = n
        self.ghosts = ghosts
        self.bufs = []

    def f32(self, a, b):
        return self.arena.ap[:, self.off + a:self.off + b]

    def bf(self, a, b):
        return self.arena.ap[:, self.off + a:self.off + b].bitcast(BF16)

    def buf(self, ap, name=''):
        b = Buf(ap, name, ghost=self.ghosts)
        self.bufs.append(b)
        return b


class Arena:
    def __init__(self, P, ap, total):
        self.P = P
        self.ap = ap
        self.free = [[0, total, []]]

    def take(self, n, name=''):
        for i, (s, sz, g) in enumerate(self.free):
            if sz >= n:
                if sz == n:
                    self.free.pop(i)
                else:
                    self.free[i] = [s + n, sz - n, g]
                return Region(self, s, n, list(g))
        raise RuntimeError("arena full allocating %s (%d words); free=%s" % (name, n, [(s, z) for s, z, _ in self.free]))

    def take_split(self, n, name=''):
        regs = []
        left = n
        while left > 0:
            chunks = sorted(self.free, key=lambda r: -r[1])
            if not chunks:
                raise RuntimeError("arena full (split) allocating %s" % name)
            sz = min(chunks[0][1], left)
            for i, (s_, z_, g_) in enumerate(self.free):
                if s_ == chunks[0][0]:
                    if z_ == sz:
                        self.free.pop(i)
                    else:
                        self.free[i] = [s_ + sz, z_ - sz, g_]
                    regs.append(Region(self, s_, sz, list(g_)))
                    break
            left -= sz
        return regs

    def drop(self, reg):
        g = self.P.retire(reg.bufs) + list(reg.ghosts)
        self.free.append([reg.off, reg.n, g])
        self.free.sort(key=lambda r: r[0])
        merged = []
        for r in self.free:
            if merged and merged[-1][0] + merged[-1][1] == r[0]:
                merged[-1][1] += r[1]
                merged[-1][2] = merged[-1][2] + r[2]
            else:
                merged.append(r)
        self.free = merged


class Ring:
    def __init__(self, items):
        self.items = items
        self.i = 0

    def next(self):
        b = self.items[self.i % len(self.items)]
        self.i += 1
        return b


def build_program():
    nc = bass.Bass("TRN2", target_bir_lowering=False)

    def din(name, shape):
        return nc.dram_tensor(name, list(shape), F32, kind="ExternalInput").ap()

    def dout(name, shape):
        return nc.dram_tensor(name, list(shape), F32, kind="ExternalOutput").ap()

    x_own = din("x_own", [NOWN * 128, D])
    x_pre = din("x_pre", [NPRE * 128, D])
    x_smp = din("x_smp", [128, D])
    mem = din("mem", [256, D])
    state = din("state", [16, 8, 128, 256])
    ck = din("ck", [16, 256, 1024])
    cv = din("cv", [16, 256, 1024])
    w_in = din("w_in", [D, IN_W])
    w_pa = din("w_pa", [1024, D])
    w_pb = din("w_pb", [2048, D])
    w_pc = din("w_pc", [1024, D])
    w_mk = din("w_mk", [D, 1024])
    w_mv = din("w_mv", [D, 1024])
    w_out = din("w_out", [D, D])
    w_fg = din("w_fg", [D, D_FF])
    w_fu = din("w_fu", [D, D_FF])
    w_fd = din("w_fd", [D_FF, D])
    sgu_wT = din("sgu_wT", [128, 2, 4, 128])
    sgu_bb = din("sgu_bb", [1, 2 * 4 * 128])
    vec_a = din("vec_a", [1, 2 * 1024])
    vec_g = din("vec_g", [1, 2048])
    vec_ln = din("vec_ln", [1, 4 * 2048])
    cst_f = din("cst_f", [128, 17 * 128 * 2 + 16 + 16 + 16])
    cst_b = din("cst_b", [128, 128 * 3 + 16 * 128])

    y_own = dout("y_own", [NOWN * 128, D])
    y_smp = dout("y_smp", [128, D])
    st_p = dout("st_p", [8, 128, 256])
    mk_o = dout("mk_o", [256, 1024])
    mv_o = dout("mv_o", [256, 1024])
    st_s = dout("st_s", [16, 8, 128, 256])
    cv_s = dout("cv_s", [128, 1024])

    def xrows(t):
        if t < NOWN:
            return x_own[t * 128:(t + 1) * 128, :]
        return x_smp[:, :]

    def yrows(t):
        if t < NOWN:
            return y_own[t * 128:(t + 1) * 128, :]
        return y_smp[:, :]

    def wcols(w, c0, n, k0=0, nk=None):
        nk = (w.shape[0] // 128 - k0) if nk is None else nk
        return w[k0 * 128:(k0 + nk) * 128, c0:c0 + n].rearrange("(k p) n -> p k n", p=128)

    with contextlib.ExitStack() as es:
        arena_t = es.enter_context(nc.sbuf_tensor("arena", [128, ARENA_WORDS], F32))
        pst = [es.enter_context(nc.psum_tensor("ps%d" % i, [128, 512], F32)) for i in range(8)]
        semctx = lambda name: es.enter_context(nc.semaphore(name))
        block = es.enter_context(nc.Block())
        P = Prog(nc)
        A = Arena(P, arena_t[:, :], ARENA_WORDS)
        _stop = _os.environ.get("MK_STOP", "")
        PS = [Buf(p[:, :], "ps%d" % i) for i, p in enumerate(pst)]
        for b_ in PS:
            b_.exclusive = True
        psring = Ring(PS)

        def psb(b):
            return b.ap

        evac_flip = [0]

        def evac(dst_ap, src_ap, reads, writes, eng=None):
            if eng is None:
                eng = 'act' if (evac_flip[0] % 2 == 0) else 'dve'
                evac_flip[0] += 1
            if eng == 'act':
                P.op('act', lambda e: e.activation(dst_ap, src_ap, AF.Copy), reads=reads, writes=writes)
            else:
                P.op('dve', lambda e: e.tensor_copy(dst_ap, src_ap), reads=reads, writes=writes)

        def mm(out_b, out_ap, l_b, l_ap, r_b, r_ap, start, stop):
            P.op('pe', lambda e: e.matmul(out_ap, l_ap, r_ap, start=start, stop=stop),
                 reads=[l_b, r_b], writes=[out_b])

        def tr(out_b, out_ap, in_b, in_ap, ident_b):
            P.op('pe', lambda e: e.transpose(out_ap, in_ap, ident_b.ap), reads=[in_b, ident_b], writes=[out_b])

        def load(queue, dst_b, dst_ap, src_ap):
            P.dma(queue, lambda e: e.dma_start(out=dst_ap, in_=src_ap), dst_b, writes=[dst_b])

        def store(queue, dst_ap, src_b, src_ap):
            P.dma(queue, lambda e: e.dma_start(out=dst_ap, in_=src_ap), src_b, reads=[src_b])

        def tt(eng, out_ap, a_ap, b_ap, op, reads, writes):
            P.op(eng, lambda e: e.tensor_tensor(out_ap, a_ap, b_ap, op), reads=reads, writes=writes)

        def ts(eng, out_ap, a_ap, s1, s2, op0, op1, reads, writes):
            if op1 is None:
                P.op(eng, lambda e: e.tensor_scalar(out_ap, a_ap, s1, None, op0), reads=reads, writes=writes)
            else:
                P.op(eng, lambda e: e.tensor_scalar(out_ap, a_ap, s1, s2, op0, op1), reads=reads, writes=writes)

        def stt(out_ap, a_ap, s, b_ap, op0, op1, reads, writes):
            P.op('dve', lambda e: e.scalar_tensor_tensor(out_ap, a_ap, s, b_ap, op0, op1), reads=reads, writes=writes)

        def act(out_ap, in_ap, func, reads, writes, bias=None, scale=None, accum=None):
            kw = {}
            if bias is not None:
                kw['bias'] = bias
            if scale is not None:
                kw['scale'] = scale
            if accum is not None:
                kw['accum_out'] = accum
            P.op('act', lambda e: e.activation(out_ap, in_ap, func, **kw), reads=reads, writes=writes)

        r_eps = A.take(2, "epsb")
        EPSB = r_eps.buf(r_eps.f32(0, 2), "epsb")
        P.op('dve', lambda e: e.memset(EPSB.ap, EPS), writes=[EPSB])
        r_cs = A.take(48, "cst_s")
        CS = r_cs.buf(r_cs.f32(0, 48), "cst_s")
        o0 = 2 * 17 * 128
        load('sp', CS, CS.ap, cst_f[:, o0:o0 + 48])
        qscale = CS.ap[:, 0:16]
        kscale = CS.ap[:, 16:32]
        rowmask = CS.ap[:, 32:48]
        cst_cos = cst_f[:, 0:17 * 128].rearrange("p (t d) -> p t d", t=17)
        cst_sin = cst_f[:, 17 * 128:2 * 17 * 128].rearrange("p (t d) -> p t d", t=17)
        NCB = (128 * 3 + 16 * 128) // 2
        r_cb = A.take(NCB, "cst_b")
        CB = r_cb.buf(r_cb.bf(0, NCB), "cst_b")
        load('pool', CB, CB.ap, cst_b[:, :])
        identb = CB
        ident_ap = CB.ap[:, 0:128]
        maskT_p = CB.ap[:, 128:256]
        maskT_s = CB.ap[:, 256:384]
        blockmask = CB.ap[:, 384:384 + 2048].rearrange("p (b i) -> p b i", b=16)

        def trn(out_b, out_ap, in_b, in_ap):
            P.op('pe', lambda e: e.matmul(out_ap, in_ap, ident_ap, start=True, stop=True), reads=[in_b, identb], writes=[out_b])

        r_xP = A.take(16 * NPRE * 128 // 2, "xTpre")
        xP_ap = r_xP.bf(0, 16 * NPRE * 128 // 2).rearrange("p (k n) -> p k n", k=16)
        xP = [r_xP.buf(xP_ap[:, :, t * 128:(t + 1) * 128], "xP%d" % t) for t in range(NPRE)]
        r_xin = A.take(8 * 1024, "xin")
        xin = Ring([r_xin.buf(r_xin.bf(i * 1024, (i + 1) * 1024), "xin%d" % i) for i in range(8)])

        def load_transpose(src_rows, dstbuf, xi=None):
            if xi is None:
                xi = xin.next()
                load('pool', xi, xi.ap, src_rows)
            for hf in range(4):
                ps = psring.next()
                pv = psb(ps)
                for k in range(4):
                    kk = hf * 4 + k
                    trn(ps, pv[:, k * 128:(k + 1) * 128], xi, xi.ap[:, kk * 128:(kk + 1) * 128])
                evac(dstbuf.ap[:, hf * 4:(hf + 1) * 4, :], pv.rearrange("p (k n) -> p k n", k=4), [ps], [dstbuf])

        r_xin2 = A.take(NT * 1024, "xin_own")
        xin_own = [r_xin2.buf(r_xin2.bf(i * 1024, (i + 1) * 1024), "xino%d" % i) for i in range(NT)]

        ROT_ENG = _os.environ.get("MK_ROT", "dve")

        def rotary(qf, nqk, cosb, cos_ap, sin_ap, t1, t2, qb):
            cs = cos_ap.to_broadcast([128, nqk, 128])
            sa = sin_ap[:, :, 0:64].to_broadcast([128, nqk, 64])
            sb_ = sin_ap[:, :, 64:128].to_broadcast([128, nqk, 64])
            tt(ROT_ENG, t1.ap[:, 0:nqk, :], qf.ap[:, 0:nqk, :], cs, ALU.mult, [qf, cosb], [t1])
            tt(ROT_ENG, t2.ap[:, 0:nqk, 0:64], qf.ap[:, 0:nqk, 64:128], sa, ALU.mult, [qf, cosb], [t2])
            tt(ROT_ENG, t2.ap[:, 0:nqk, 64:128], qf.ap[:, 0:nqk, 0:64], sb_, ALU.mult, [qf, cosb], [t2])
            tt(ROT_ENG, qb.ap[:, 0:nqk, :], t1.ap[:, 0:nqk, :], t2.ap[:, 0:nqk, :], ALU.add, [t1, t2], [qb])

        if _stop == "0a":
            P.emit(block, semctx)
            return nc
        def run_pipeline(items, stages, pre=None, post=None):
            maxd = max(d for d, _ in stages)
            for step in range(len(items) + maxd):
                if pre:
                    pre(step)
                for d, fn in stages:
                    i = step - d
                    if 0 <= i < len(items):
                        fn(items[i])
                if post:
                    post(step)

        r_T = A.take(2048, "Tall")
        Tall = r_T.buf(r_T.f32(0, 2048).rearrange("p (h e) -> p h e", h=8), "Tall")
        P.op('dve', lambda e: e.memset(Tall.ap, 0.0), writes=[Tall])
        r_cp = A.take(2 * 8 * 128, "cfpre")
        CFP = r_cp.buf(r_cp.f32(0, 2048).rearrange("p (w t d) -> p w t d", w=2, t=8), "cfpre")
        load('sp', CFP, CFP.ap[:, 0, :, :], cst_cos[:, 0:8, :])
        load('sp', CFP, CFP.ap[:, 1, :, :], cst_sin[:, 0:8, :])
        r_wkv = A.take(2 * 3072, "wkv")
        wkvr = Ring([r_wkv.buf(r_wkv.bf(i * 3072, (i + 1) * 3072).rearrange("p (k n) -> p k n", k=16), "wkv%d" % i)
                     for i in range(2)])
        r_b0 = A.take(2 * 128 + 128 + 128 + 3 * 64 + 3 * 128, "b0work")
        kf0 = Ring([r_b0.buf(r_b0.f32(i * 128, (i + 1) * 128).rearrange("p (j d) -> p j d", j=1), "kf0_%d" % i) for i in range(2)])
        t10 = r_b0.buf(r_b0.f32(256, 384).rearrange("p (j d) -> p j d", j=1), "t10")
        t20 = r_b0.buf(r_b0.f32(384, 512).rearrange("p (j d) -> p j d", j=1), "t20")
        kb0 = Ring([r_b0.buf(r_b0.bf(512 + i * 64, 512 + (i + 1) * 64).rearrange("p (j d) -> p j d", j=1), "kb0_%d" % i) for i in range(3)])
        vb0 = Ring([r_b0.buf(r_b0.bf(704 + i * 128, 704 + (i + 1) * 128), "vb0_%d" % i) for i in range(3)])
        r_wh0 = A.take(6144, "wh0")
        r_wh1 = A.take(6144, "wh1")
        whs = [r_wh0.buf(r_wh0.bf(0, 6144).rearrange("p (k n) -> p k n", k=16), "wh0"),
               r_wh1.buf(r_wh1.bf(0, 6144).rearrange("p (k n) -> p k n", k=16), "wh1")]

        def load_wh(h):
            wh = whs[h % 2]
            load('pool', wh, wh.ap[:, :, 0:128], wcols(w_in, OFF_RQ + h * 128, 128))
            load('pool', wh, wh.ap[:, :, 128:256], wcols(w_in, OFF_RK + h * 128, 128))
            load('pool', wh, wh.ap[:, :, 256:512], wcols(w_in, OFF_RV + h * 256, 256))
            load('pool', wh, wh.ap[:, :, 512:768], wcols(w_in, OFF_RG + h * 256, 256))

        ringA0 = Ring([PS[0], PS[1], PS[2], PS[3]])
        ringD0 = Ring([PS[4], PS[5], PS[6], PS[7]])
        b0ctx = {}
        b0w = {}

        def b0_pre(step):
            if step < NPRE:
                load_transpose(x_pre[step * 128:(step + 1) * 128, :], xP[step])
            if step == 2:
                for t_ in range(NT):
                    load('pool', xin_own[t_], xin_own[t_].ap, xrows(t_))
            if step == 36:
                load_wh(0)
            if step == 56:
                load_wh(1)
            if step % NPRE == 0 and step // NPRE < 8:
                h = step // NPRE
                wkv = wkvr.next()
                load('pool', wkv, wkv.ap[:, :, 0:128], wcols(w_in, OFF_RK + h * 128, 128))
                load('pool', wkv, wkv.ap[:, :, 128:384], wcols(w_in, OFF_RV + h * 256, 256))
                b0w[h] = wkv

        def b0_s0(it):
            h, t = it
            wkv = b0w[h]
            psA = ringA0.next()
            for k in range(16):
                mm(psA, psA.ap[:, 0:384], xP[t], xP[t].ap[:, k, :], wkv, wkv.ap[:, k, :], k == 0, k == 15)
            kf = kf0.next()
            act(kf.ap[:, 0, :], psA.ap[:, 0:128], AF.Copy, [psA, CS], [kf], scale=kscale[:, h:h + 1])
            vb = vb0.next()
            evac(vb.ap, psA.ap[:, 128:384], [psA], [vb], eng='act')
            kb = kb0.next()
            rotary(kf, 1, CFP, CFP.ap[:, 0, t:t + 1, :], CFP.ap[:, 1, t:t + 1, :], t10, t20, kb)
            b0ctx[it] = (kb, vb)

        def b0_s1(it):
            h, t = it
            kb, vb = b0ctx.pop(it)
            c_p = math.exp(128.0 * math.log1p(-(2.0 ** (-5.0 - h))))
            psD = ringD0.next()
            mm(psD, psD.ap[:, 0:256], kb, kb.ap[:, 0, :], vb, vb.ap, True, True)
            stt(Tall.ap[:, h, :], Tall.ap[:, h, :], c_p, psD.ap[:, 0:256], ALU.mult, ALU.add, [Tall, psD], [Tall])

        run_pipeline([(h, t) for h in range(8) for t in range(NPRE)], [(0, b0_s0), (1, b0_s1)], pre=b0_pre)
        A.drop(r_b0)
        A.drop(r_wkv)
        A.drop(r_cp)
        A.drop(r_xP)

        if _stop == "B0":
            P.emit(block, semctx)
            return nc
        r_xT = A.take(16 * NTOK // 2, "xT")
        xT_ap = r_xT.bf(0, 16 * NTOK // 2).rearrange("p (k n) -> p k n", k=16)
        xT = [r_xT.buf(xT_ap[:, :, t * 128:(t + 1) * 128], "xT%d" % t) for t in range(NT)]
        for t in range(NT):
            load_transpose(None, xT[t], xi=xin_own[t])
        A.drop(r_xin)
        A.drop(r_xin2)

        if _stop == "0b":
            P.emit(block, semctx)
            return nc
        r_bT = A.take(16 * NTOK // 2, "bT")
        bT_ap = r_bT.bf(0, 16 * NTOK // 2).rearrange("p (k n) -> p k n", k=16)
        bT = [r_bT.buf(bT_ap[:, :, t * 128:(t + 1) * 128], "bT%d" % t) for t in range(NT)]
        r_co = A.take(2 * 9 * 128, "cfown")
        CFO = r_co.buf(r_co.f32(0, 2304).rearrange("p (w t d) -> p w t d", w=2, t=9), "cfown")
        load('sp', CFO, CFO.ap[:, 0, :, :], cst_cos[:, 8:17, :])
        load('sp', CFO, CFO.ap[:, 1, :, :], cst_sin[:, 8:17, :])
        r_gng = A.take(2 * 256, "gng")
        gngr = Ring([r_gng.buf(r_gng.f32(i * 256, (i + 1) * 256), "gng%d" % i) for i in range(2)])
        r_s0 = A.take(4096, "s0")
        s0 = r_s0.buf(r_s0.f32(0, 4096).rearrange("p (b e) -> p b e", b=16), "s0")
        r_bs = A.take_split(12100, "bwork")
        curs = [0 for _ in r_bs]

        def bw(nwords, dtype, name, shape=None, **kw):
            for i_, r_b in enumerate(r_bs):
                if curs[i_] + nwords <= r_b.n:
                    a0 = curs[i_]
                    curs[i_] += nwords
                    ap = r_b.f32(a0, a0 + nwords) if dtype == F32 else r_b.bf(a0, a0 + nwords)
                    if shape:
                        ap = ap.rearrange(shape, **kw)
                    return r_b.buf(ap, name)
            raise RuntimeError("bwork full at %s" % name)

        s0bf = bw(2048, BF16, "s0bf", "p (b e) -> p b e", b=16)
        qmk = bw(1024, BF16, "qmaskT", "p (b i) -> p b i", b=16)
        kmk = bw(1024, BF16, "kmaskR", "p (b d) -> p b d", b=16)
        qk_f = Ring([bw(256, F32, "qkf%d" % i, "p (j d) -> p j d", j=2) for i in range(2)])
        t1_f = bw(256, F32, "t1f", "p (j d) -> p j d", j=2)
        t2_f = bw(256, F32, "t2f", "p (j d) -> p j d", j=2)
        qk_b = Ring([bw(128, BF16, "qkb%d" % i, "p (j d) -> p j d", j=2) for i in range(7)])
        qkT = Ring([bw(128, BF16, "qkT%d" % i, "p (j d) -> p j d", j=2) for i in range(5)])
        v_b = Ring([bw(128, BF16, "vb%d" % i) for i in range(7)])
        scm = Ring([bw(64, BF16, "scm%d" % i) for i in range(3)])
        S_bf = Ring([bw(128, BF16, "Sbf%d" % i) for i in range(2)])
        sg_f = [bw(256, F32, "sgf%d" % t) for t in range(NT)]
        gst = Ring([bw(12, F32, "gst%d" % i) for i in range(2)])
        on_f = bw(256, F32, "onf")
        bg_b = Ring([bw(128, BF16, "bgb%d" % i) for i in range(4)])
        stmp = bw(512, F32, "stmp", "p (b e) -> p b e", b=2)
        vb_smp = bw(128, BF16, "vbsmp")
        deferred = []

        BD = [int(x) for x in _os.environ.get('MK_BD', '1,2,3,4').split(',')]
        ringA = Ring([PS[0], PS[1]])
        ringT = Ring([PS[2], PS[3], PS[4]])
        ringO = Ring([PS[5], PS[6]])
        ringD = Ring([PS[7]])
        ringDS = Ring([PS[7], PS[5], PS[6]])
        bctx = {}
        bhead = {}
        Sbcur = [None]

        def hconst(h):
            lg = math.log1p(-(2.0 ** (-5.0 - h)))
            return math.exp(128.0 * lg), math.exp(8.0 * lg)

        def b_pre(step):
            if step % NT == 0 and step // NT < 8:
                h = step // NT
                if h >= 2:
                    load_wh(h)
                bhead[h] = whs[h % 2]
            if step == NT * 7 + 4:
                load('pool', whs[0], whs[0].ap[:, :, 0:768], wcols(w_in, OFF_AV, 768))
            if step % NT == min(BD[2], NT - 1) and step // NT < 8:
                h = step // NT
                c_p, c_s = hconst(h)
                wh = bhead[h]
                gng = gngr.next()
                load('sp', gng, gng.ap, vec_g[0:1, h * 256:(h + 1) * 256].partition_broadcast(128))
                Sb = S_bf.next()
                act(Sb.ap, Tall.ap[:, h, :], AF.Copy, [Tall], [Sb], scale=c_p)
                Sbcur[0] = Sb
                for t in range(NT):
                    psG = ringA.next()
                    for k in range(16):
                        mm(psG, psG.ap[:, 0:256], xT[t], xT[t].ap[:, k, :], wh, wh.ap[:, k, 512:768], k == 0, k == 15)
                    act(sg_f[t].ap, psG.ap[:, 0:256], AF.Silu, [psG], [sg_f[t]])
                    tt(ROT_ENG, sg_f[t].ap, sg_f[t].ap, gng.ap, ALU.mult, [sg_f[t], gng], [sg_f[t]])

        def b_post(step):
            if step == 0:
                load('sp', s0, s0.ap, state[:, 0, :, :].rearrange("b d e -> d b e"))
            if deferred:
                deferred.pop(0)()

        def b_s0(it):
            h, t = it
            wh = bhead[h]
            smp = (t == NOWN)
            xb = xT[t]
            psA = ringA.next()
            for k in range(16):
                mm(psA, psA.ap[:, 0:512], xb, xb.ap[:, k, :], wh, wh.ap[:, k, 0:512], k == 0, k == 15)
            which = 8 if smp else 0
            qf = qk_f.next()
            act(qf.ap[:, 0, :], psA.ap[:, 0:128], AF.Copy, [psA, CS], [qf], scale=qscale[:, which + h:which + h + 1])
            act(qf.ap[:, 1, :], psA.ap[:, 128:256], AF.Copy, [psA, CS], [qf], scale=kscale[:, which + h:which + h + 1])
            vb = vb_smp if smp else v_b.next()
            evac(vb.ap, psA.ap[:, 256:512], [psA], [vb], eng='act')
            qb = qk_b.next()
            if _os.environ.get("MK_ABL", "") == "rot":
                evac(qb.ap, qf.ap, [qf], [qb], eng='act')
            else:
                rotary(qf, 2, CFO, CFO.ap[:, 0, t:t + 1, :], CFO.ap[:, 1, t:t + 1, :], t1_f, t2_f, qb)
            bctx[it] = dict(vb=vb, qb=qb)

        def b_s1(it):
            c = bctx[it]
            qb = c['qb']
            psT = ringT.next()
            pv = psb(psT)
            trn(psT, pv[:, 0:128], qb, qb.ap[:, 0, :])
            trn(psT, pv[:, 128:256], qb, qb.ap[:, 1, :])
            qT = qkT.next()
            evac(qT.ap, pv[:, 0:256].rearrange("p (j d) -> p j d", j=2), [psT], [qT], eng='act')
            c['qT'] = qT

        def b_s2(it):
            h, t = it
            c = bctx[it]
            qT = c['qT']
            smp = (t == NOWN)
            psS = ringT.next()
            mm(psS, psS.ap[:, 0:128], qT, qT.ap[:, 1, :], qT, qT.ap[:, 0, :], True, True)
            sm = scm.next()
            tt('dve', sm.ap, psS.ap[:, 0:128], maskT_s if smp else maskT_p, ALU.mult, [psS, CB], [sm])
            c['sm'] = sm
            if smp:
                tt('dve', qmk.ap, qT.ap[:, 0:1, :].to_broadcast([128, 16, 128]), blockmask, ALU.mult, [qT, CB], [qmk])
                evac(s0bf.ap, s0.ap, [s0], [s0bf], eng='act')

        def b_s3(it):
            h, t = it
            c = bctx[it]
            qT, sm, vb, qb = c['qT'], c['sm'], c['vb'], c['qb']
            smp = (t == NOWN)
            c_p, c_s = hconst(h)
            psO = ringO.next()
            if not smp:
                Sb = Sbcur[0]
                mm(psO, psO.ap[:, 0:256], sm, sm.ap, vb, vb.ap, True, False)
                mm(psO, psO.ap[:, 0:256], qT, qT.ap[:, 0, :], Sb, Sb.ap, False, True)
                psD = ringD.next()
                mm(psD, psD.ap[:, 0:256], qb, qb.ap[:, 1, :], vb, vb.ap, True, True)
                stt(Tall.ap[:, h, :], Tall.ap[:, h, :], c_p, psD.ap[:, 0:256], ALU.mult, ALU.add, [Tall, psD], [Tall])
                if t < NOWN - 1:
                    Sb = S_bf.next()
                    act(Sb.ap, Tall.ap[:, h, :], AF.Copy, [Tall], [Sb], scale=c_p)
                    Sbcur[0] = Sb
                else:
                    act(Tall.ap[:, h, :], Tall.ap[:, h, :], AF.Copy, [Tall], [Tall], scale=c_p)
            else:
                mm(psO, psO.ap[:, 0:256], sm, sm.ap, vb, vb.ap, True, False)
                for b in range(16):
                    mm(psO, psO.ap[:, 0:256], qmk, qmk.ap[:, b, :], s0bf, s0bf.ap[:, b, :], False, b == 15)
            _abl = _os.environ.get("MK_ABL", "")
            if _abl == "gn":
                bg = bg_b.next()
                tt('dve', bg.ap, psO.ap[:, 0:256], sg_f[t].ap, ALU.mult, [psO, sg_f[t]], [bg])
                c['bg'] = bg
                return
            gs_ = gst.next()
            P.op('dve', lambda e, a=gs_.ap[:, 0:6], b=psO.ap[:, 0:256]: e.bn_stats(a, b), reads=[psO], writes=[gs_])
            P.op('dve', lambda e, a=gs_.ap[:, 6:8], b=gs_.ap[:, 0:6]: e.bn_aggr(a, b), reads=[gs_], writes=[gs_])
            act(gs_.ap[:, 9:10], gs_.ap[:, 7:8], AF.Sqrt, [gs_, EPSB], [gs_], bias=EPSB.ap[:, 0:1])
            P.op('dve', lambda e, a=gs_.ap[:, 10:11], b=gs_.ap[:, 9:10]: e.reciprocal(a, b), reads=[gs_], writes=[gs_])
            ts('dve', on_f.ap, psO.ap[:, 0:256], gs_.ap[:, 6:7], gs_.ap[:, 10:11], ALU.subtract, ALU.mult, [psO, gs_], [on_f])
            bg = bg_b.next()
            tt('dve', bg.ap, on_f.ap, sg_f[t].ap, ALU.mult, [on_f, sg_f[t]], [bg])
            c['bg'] = bg

        def b_s4(it):
            h, t = it
            c = bctx.pop(it)
            bg, vb, qb = c['bg'], c['vb'], c['qb']
            smp = (t == NOWN)
            c_p, c_s = hconst(h)
            psT2 = ringT.next()
            pv2 = psb(psT2)
            trn(psT2, pv2[:, 0:128], bg, bg.ap[:, 0:128])
            trn(psT2, pv2[:, 128:256], bg, bg.ap[:, 128:256])
            evac(bT[t].ap[:, 2 * h:2 * h + 2, :], pv2[:, 0:256].rearrange("p (j d) -> p j d", j=2), [psT2], [bT[t]], eng='act')
            if smp:
                tt('dve', kmk.ap, qb.ap[:, 1:2, :].to_broadcast([128, 16, 128]),
                   rowmask.unsqueeze(2).to_broadcast([128, 16, 128]), ALU.mult, [qb, CS], [kmk])
                def chunk(bps, last, h=h, vb=vb, c_s=c_s):
                    for bp in bps:
                        psD = ringDS.next()
                        for j in range(2):
                            b = bp * 2 + j
                            mm(psD, psD.ap[:, j * 256:(j + 1) * 256], kmk, kmk.ap[:, b, :], vb, vb.ap, True, True)
                        tt('dve', stmp.ap, psD.ap.rearrange("p (b e) -> p b e", b=2), s0.ap[:, bp * 2:bp * 2 + 2, :],
                           ALU.add, [psD, s0], [stmp])
                        act(s0.ap[:, bp * 2:bp * 2 + 2, :], stmp.ap, AF.Copy, [stmp], [s0], scale=c_s)
                    if last:
                        store('sp', st_s[:, h, :, :].rearrange("b d e -> d b e"), s0, s0.ap)
                        if h + 1 < 8:
                            load('sp', s0, s0.ap, state[:, h + 1, :, :].rearrange("b d e -> d b e"))
                chunk(list(range(8)), True)

        run_pipeline([(h, t) for h in range(8) for t in range(NT)],
                     [(0, b_s0), (BD[0], b_s1), (BD[1], b_s2), (BD[2], b_s3), (BD[3], b_s4)], pre=b_pre, post=b_post)
        while deferred:
            deferred.pop(0)()
        store('sp', st_p.rearrange("h d e -> d h e"), Tall, Tall.ap)
        for r_ in r_bs:
            A.drop(r_)
        A.drop(r_wh1)
        A.drop(r_s0)
        A.drop(r_gng)
        A.drop(r_co)
        A.drop(r_T)
        A.drop(r_cs)

        if _stop == "B":
            P.emit(block, semctx)
            return nc
        r_aT = A.take(8 * NTOK // 2, "aT")
        aT_ap = r_aT.bf(0, 8 * NTOK // 2).rearrange("p (k n) -> p k n", k=8)
        aT = r_aT.buf(aT_ap, "aT")
        WVa = whs[0]
        r_wvb = A.take(2048, "wvb")
        WVb = r_wvb.buf(r_wvb.bf(0, 2048).rearrange("p (k n) -> p k n", k=16), "wvb")
        load('pool', WVb, WVb.ap, wcols(w_in, OFF_AV + 768, 256))
        r_at = A.take(2048 + 1024 + 1024 + 512, "atab")
        LNA = r_at.buf(r_at.f32(0, 2048), "lna")
        load('sp', LNA, LNA.ap, vec_a[0:1, :].partition_broadcast(128))
        BSB = r_at.buf(r_at.f32(2048, 3072).rearrange("p (w g i) -> p w g i", w=2, g=4), "bsb")
        load('sp', BSB, r_at.f32(2048, 3072), sgu_bb[0:1, :].partition_broadcast(128))
        WSF = r_at.buf(r_at.f32(3072, 4096).rearrange("p (w g i) -> p w g i", w=2, g=4), "wsf")
        load('sp', WSF, r_at.f32(3072, 4096), sgu_wT.rearrange("j w g i -> j (w g i)"))
        WSB = r_at.buf(r_at.bf(4096, 4608).rearrange("p (w g i) -> p w g i", w=2, g=4), "wsb")
        tt('dve', WSB.ap[:, 0, :, :], WSF.ap[:, 0, :, :], maskT_p.unsqueeze(1).to_broadcast([128, 4, 128]), ALU.mult,
           [WSF, CB], [WSB])
        tt('dve', WSB.ap[:, 1, :, :], WSF.ap[:, 1, :, :], maskT_s.unsqueeze(1).to_broadcast([128, 4, 128]), ALU.mult,
           [WSF, CB], [WSB])
        r_zTs = [A.take(4 * NTOK, "zT%d" % i) for i in range(2)]
        zT_aps = [r_.f32(0, 4 * NTOK).rearrange("p (k n) -> p k n", k=4) for r_ in r_zTs]
        zTh = [[r_zTs[i].buf(zT_aps[i][:, :, t * 128:(t + 1) * 128], "zT%d_%d" % (i, t)) for t in range(NT)] for i in range(2)]
        r_aw = A.take(2 * 1024 + 2 * 512 + 64 + 512, "awork")
        vg_items = []
        for i in range(2):
            full = Buf(r_aw.f32(i * 1024, (i + 1) * 1024), "vgfull%d" % i)
            parts = [r_aw.buf(r_aw.f32(i * 1024 + g * 256, i * 1024 + (g + 1) * 256), "vg%d_%d" % (i, g)) for g in range(4)]
            vg_items.append((full, parts))
        vg = Ring(vg_items)
        vlb = Ring([r_aw.buf(r_aw.bf(2048 + i * 512, 2048 + (i + 1) * 512), "vlb%d" % i) for i in range(2)]
                   + [r_aw.buf(r_aw.bf(3136, 3648), "vlb2")])
        ast = Buf(r_aw.f32(3072, 3072 + 24).rearrange("p (g s) -> p g s", g=4), "ast")
        amv = Buf(r_aw.f32(3072 + 24, 3072 + 32).rearrange("p (g s) -> p g s", g=4), "amv")
        ast_g = [r_aw.buf(r_aw.f32(3072 + 6 * g, 3072 + 6 * g + 6), "ast%d" % g) for g in range(4)]
        amv_g = [r_aw.buf(r_aw.f32(3072 + 24 + 2 * g, 3072 + 26 + 2 * g), "amv%d" % g) for g in range(4)]
        atm = r_aw.buf(r_aw.f32(3072 + 32, 3072 + 36), "atm")
        ars = r_aw.buf(r_aw.f32(3072 + 36, 3072 + 40), "ars")
        anm = r_aw.buf(r_aw.f32(3072 + 40, 3072 + 44), "anm")
        actx = {}

        def a_s0(t):
            pss = [psring.next(), psring.next()]
            for k in range(16):
                mm(pss[0], pss[0].ap, xT[t], xT[t].ap[:, k, :], WVa, WVa.ap[:, k, 0:512], k == 0, k == 15)
            for k in range(16):
                mm(pss[1], pss[1].ap[:, 0:256], xT[t], xT[t].ap[:, k, :], WVa, WVa.ap[:, k, 512:768], k == 0, k == 15)
            for k in range(16):
                mm(pss[1], pss[1].ap[:, 256:512], xT[t], xT[t].ap[:, k, :], WVb, WVb.ap[:, k, :], k == 0, k == 15)
            v, vgs = vg.next()
            for n in range(2):
                act(v.ap[:, n * 512:(n + 1) * 512], pss[n].ap, AF.Gelu_apprx_tanh, [pss[n]], [vgs[2 * n], vgs[2 * n + 1]])
            for g in range(4):
                P.op('dve', lambda e, a=ast.ap[:, g, :], b=v.ap[:, g * 256:(g + 1) * 256]: e.bn_stats(a, b),
                     reads=[vgs[g]], writes=[ast_g[g]])
            for g in range(4):
                P.op('dve', lambda e, a=amv.ap[:, g, :], b=ast.ap[:, g, :]: e.bn_aggr(a, b), reads=[ast_g[g]], writes=[amv_g[g]])
            act(atm.ap, amv.ap[:, :, 1], AF.Sqrt, amv_g + [EPSB], [atm], bias=EPSB.ap[:, 0:1])
            P.op('dve', lambda e: e.reciprocal(ars.ap, atm.ap), reads=[atm], writes=[ars])
            stt(anm.ap, amv.ap[:, :, 0], -1.0, ars.ap, ALU.mult, ALU.mult, amv_g + [ars], [anm])
            for g in range(4):
                act(v.ap[:, g * 256:(g + 1) * 256], v.ap[:, g * 256:(g + 1) * 256], AF.Identity, [vgs[g], ars, anm], [vgs[g]],
                    bias=anm.ap[:, g:g + 1], scale=ars.ap[:, g:g + 1])
            tt('dve', v.ap, v.ap, LNA.ap[:, 0:1024], ALU.mult, vgs + [LNA], vgs)
            tt('dve', v.ap, v.ap, LNA.ap[:, 1024:2048], ALU.add, vgs + [LNA], vgs)
            if t == NOWN:
                P.dma('sp', lambda e: e.dma_start(out=cv_s[:, :], in_=v.ap), vgs[0], reads=vgs)
            vb = vlb.next()
            evac(vb.ap, v.ap, vgs, [vb], eng='act')
            actx[t] = vb

        def a_s1(t):
            w = 1 if t == NOWN else 0
            vb = actx.pop(t)
            for hf in range(2):
                ps = psring.next()
                for cc in range(4):
                    c = hf * 4 + cc
                    mm(ps, ps.ap[:, cc * 128:(cc + 1) * 128], vb, vb.ap[:, c * 128:(c + 1) * 128], WSB, WSB.ap[:, w, c // 2, :],
                       True, True)
                tt('dve', zTh[hf][t].ap.rearrange("p (g j) i -> p g j i", g=2),
                   ps.ap.rearrange("p (g j i) -> p g j i", g=2, j=2),
                   BSB.ap[:, w, hf * 2:hf * 2 + 2, :].unsqueeze(2).to_broadcast([128, 2, 2, 128]),
                   ALU.add, [ps, BSB], [zTh[hf][t]])

        run_pipeline(list(range(NT)), [(0, a_s0), (1, a_s1)])
        A.drop(r_wvb)
        A.drop(r_wh0)
        r_memin = A.take(2048, "memin")
        memin = r_memin.buf(r_memin.bf(0, 2048).rearrange("p (t f) -> p t f", t=2), "memin")
        load('pool', memin, memin.ap, mem.rearrange("(t p) f -> p t f", p=128))
        r_wk0 = A.take(4096, "wk0")
        wk0 = r_wk0.buf(r_wk0.bf(0, 4096).rearrange("p (k n) -> p k n", k=16), "wkp0")
        load('pool', wk0, wk0.ap, wcols(w_mk, 0, 512))
        r_wus = [A.take(1024, "wu%d" % i) for i in range(3)]
        wur = Ring([r_wus[i].buf(r_wus[i].bf(0, 1024).rearrange("p (k n) -> p k n", k=16), "wu%d" % i) for i in range(3)])
        r_ugs = [A.take(512, "ug%d" % i) for i in range(2)]
        ugr = Ring([r_ugs[i].buf(r_ugs[i].f32(0, 512), "ug%d" % i) for i in range(2)])

        def xT_rd(o, n):
            return [xT[t] for t in range(o // 128, (o + n) // 128)]

        def tl(lst, o, n):
            return [lst[t] for t in range(o // 128, (o + n) // 128)]

        for c in range(8):
            wu = wur.next()
            load('pool', wu, wu.ap, wcols(w_in, OFF_AU + c * 128, 128))
            for (o, n) in TG:
                ps = psring.next()
                for k in range(16):
                    P.op('pe', lambda e, a=ps.ap[:, 0:n], l=wu.ap[:, k, :], r=xT_ap[:, k, o:o + n], s=(k == 0), p=(k == 15):
                         e.matmul(a, l, r, start=s, stop=p), reads=[wu] + xT_rd(o, n), writes=[ps])
                ug = ugr.next()
                act(ug.ap[:, 0:n], ps.ap[:, 0:n], AF.Gelu_apprx_tanh, [ps], [ug])
                tt('dve', aT_ap[:, c, o:o + n], ug.ap[:, 0:n], zT_aps[c // 4][:, c % 4, o:o + n], ALU.mult,
                   [ug] + tl(zTh[c // 4], o, n), [aT])
        for r_ in r_zTs:
            A.drop(r_)
        A.drop(r_aw)
        A.drop(r_at)
        for r_ in r_wus + r_ugs:
            A.drop(r_)

        if _stop == "A":
            P.emit(block, semctx)
            return nc
        r_cT = A.take(8 * NTOK // 2, "cT")
        cT_ap = r_cT.bf(0, 8 * NTOK // 2).rearrange("p (k n) -> p k n", k=8)
        cT = [r_cT.buf(cT_ap[:, :, t * 128:(t + 1) * 128], "cT%d" % t) for t in range(NT)]
        r_qm = A.take(8 * NTOK // 2, "qmT")
        qm_ap = r_qm.bf(0, 8 * NTOK // 2).rearrange("p (k n) -> p k n", k=8)
        qmT = r_qm.buf(qm_ap, "qmT")
        r_cw = A.take(2048 + 4096 + 2048, "cwork")
        r_ckv = A.take(2048, "ckv")
        memT = r_cw.buf(r_cw.bf(0, 2048).rearrange("p (k m) -> p k m", k=16), "memT")
        wkr = Ring([wk0, r_cw.buf(r_cw.bf(2048, 6144).rearrange("p (k n) -> p k n", k=16), "wkp1")])
        mstage = Ring([r_cw.buf(r_cw.f32(6144 + i * 1024, 6144 + (i + 1) * 1024).rearrange("p (t n) -> p t n", t=2),
                                "mst%d" % i) for i in range(2)])
        kTp = r_ckv.buf(r_ckv.bf(0, 1024).rearrange("p (k m) -> p k m", k=8), "kTp")
        vp = r_ckv.buf(r_ckv.bf(1024, 2048).rearrange("p (t n) -> p t n", t=2), "vp")
        for mt in range(2):
            for hf in range(4):
                ps = psring.next()
                pv = psb(ps)
                for k in range(4):
                    kk = hf * 4 + k
                    trn(ps, pv[:, k * 128:(k + 1) * 128], memin, memin.ap[:, mt, kk * 128:(kk + 1) * 128])
                evac(memT.ap[:, hf * 4:(hf + 1) * 4, mt * 128:(mt + 1) * 128], pv.rearrange("p (k n) -> p k n", k=4),
                     [ps], [memT])
        if _stop == "C0a":
            P.emit(block, semctx)
            return nc
        for which, (wsrc, dsto) in enumerate(((w_mk, mk_o), (w_mv, mv_o))):
            for n in range(2):
                wk = wkr.next()
                if (which, n) != (0, 0):
                    load('pool', wk, wk.ap, wcols(wsrc, n * 512, 512))
                ms = mstage.next()
                for mt in range(2):
                    ps = psring.next()
                    for k in range(16):
                        mm(ps, ps.ap, memT, memT.ap[:, k, mt * 128:(mt + 1) * 128], wk, wk.ap[:, k, :], k == 0, k == 15)
                    evac(ms.ap[:, mt, :], ps.ap, [ps], [ms])
                    if which == 1:
                        evac(vp.ap[:, mt, n * 512:(n + 1) * 512], ps.ap, [ps], [vp])
                store('sp', dsto[:, n * 512:(n + 1) * 512].rearrange("(t p) n -> p t n", p=128), ms, ms.ap)
                if which == 0:
                    for cc in range(4):
                        ps = psring.next()
                        for k in range(16):
                            mm(ps, ps.ap[:, 0:256], wk, wk.ap[:, k, cc * 128:(cc + 1) * 128], memT, memT.ap[:, k, :], k == 0, k == 15)
                        evac(kTp.ap[:, n * 4 + cc, :], ps.ap[:, 0:256], [ps], [kTp])
        if _stop == "C1":
            P.emit(block, semctx)
            return nc
        A.drop(r_cw)
        A.drop(r_memin)
        A.drop(r_wk0)
        r_wq = A.take(3 * 1024, "wq")
        wqr = Ring([r_wq.buf(r_wq.bf(i * 1024, (i + 1) * 1024).rearrange("p (k n) -> p k n", k=16), "wq%d" % i)
                    for i in range(3)])
        for c in range(8):
            wq = wqr.next()
            load('pool', wq, wq.ap, wcols(w_in, OFF_MQ + c * 128, 128))
            for (o, n) in TG:
                ps = psring.next()
                for k in range(16):
                    P.op('pe', lambda e, a=ps.ap[:, 0:n], l=wq.ap[:, k, :], r=xT_ap[:, k, o:o + n], s=(k == 0), p=(k == 15):
                         e.matmul(a, l, r, start=s, stop=p), reads=[wq] + xT_rd(o, n), writes=[ps])
                evac(qm_ap[:, c, o:o + n], ps.ap[:, 0:n], [ps], [qmT])
        if _stop == "C2":
            P.emit(block, semctx)
            return nc
        A.drop(r_wq)
        r_sk = A.take(8 * 1024 + 2 * 1024 + 2 * 512, "skv")
        kin = Ring([r_sk.buf(r_sk.bf(i * 1024, (i + 1) * 1024).rearrange("p (t f) -> p t f", t=2), "kin%d" % i)
                    for i in range(6)])
        kTb = Ring([r_sk.buf(r_sk.bf(8192 + i * 1024, 8192 + (i + 1) * 1024).rearrange("p (k m) -> p k m", k=8), "kTb%d" % i)
                    for i in range(2)])
        qmb = Ring([r_sk.buf(r_sk.bf(10240 + i * 512, 10240 + (i + 1) * 512).rearrange("p (k n) -> p k n", k=8), "qmb%d" % i)
                    for i in range(2)])
        vin = Ring([r_sk.buf(r_sk.bf((6 + i) * 1024, (7 + i) * 1024).rearrange("p (t f) -> p t f", t=2), "vin%d" % i)
                    for i in range(2)])
        KPRE, VPRE = 6, 2
        kpre = []
        for b in range(KPRE):
            ki = kin.next()
            load('pool', ki, ki.ap, ck[b].rearrange("(t p) f -> p t f", p=128))
            kpre.append(ki)
        vpre = []
        for b in range(VPRE):
            vi = vin.next()
            load('pool', vi, vi.ap, cv[b].rearrange("(t p) f -> p t f", p=128))
            vpre.append(vi)
        SC = 1.0 / 16.0

        def softmax_p(ps_s, hh):
            st = sst.ap[:, hh, :]
            P.op('dve', lambda e: e.reduce_max(st[:, 0:1], ps_s.ap[:, 0:256], AX.X), reads=[ps_s], writes=[sst])
            ts('dve', st[:, 1:2], st[:, 0:1], -SC, None, ALU.mult, None, [sst], [sst])
            pe_ = pex.next()
            P.op('dve', lambda e: e.memset(st[:, 2:3], 0.0), reads=[], writes=[sst])
            act(pe_.ap, ps_s.ap[:, 0:256], AF.Exp, [ps_s, sst], [pe_, sst], bias=st[:, 1:2], scale=SC, accum=st[:, 2:3])
            P.op('dve', lambda e: e.reciprocal(st[:, 3:4], st[:, 2:3]), reads=[sst], writes=[sst])
            pb = pbf.next()
            ts('dve', pb.ap, pe_.ap, st[:, 3:4], None, ALU.mult, None, [pe_, sst], [pb])
            return pb

        def transpose_p(pb, pT):
            psT = psring.next()
            pv = psb(psT)
            trn(psT, pv[:, 0:128], pb, pb.ap[:, 0:128])
            trn(psT, pv[:, 128:256], pb, pb.ap[:, 128:256])
            evac(pT.ap, pv[:, 0:256].rearrange("p (t n) -> p t n", t=2), [psT], [pT])

        def softmax_to_pT(ps_s, hh, pT):
            transpose_p(softmax_p(ps_s, hh), pT)

        r_pa = ([A.take(1024, "pe4_%d" % i) for i in range(2)] + [A.take(512, "pb4_%d" % i) for i in range(2)]
                + [A.take(512, "pT4_%d" % i) for i in range(2)] + [A.take(32, "st4")])
        pe4 = Ring([r_pa[i].buf(r_pa[i].f32(0, 1024).rearrange("p (h m) -> p h m", h=4), "pe4_%d" % i) for i in range(2)])
        pb4 = Ring([r_pa[2 + i].buf(r_pa[2 + i].bf(0, 512).rearrange("p (h m) -> p h m", h=4), "pb4_%d" % i) for i in range(2)])
        pT4 = Ring([r_pa[4 + i].buf(r_pa[4 + i].bf(0, 512).rearrange("p (h t n) -> p h t n", h=4, t=2), "pT4_%d" % i)
                    for i in range(2)])
        st4 = Ring([(r_pa[6].buf(r_pa[6].f32(16 * i, 16 * i + 8), "st4a%d" % i),
                     r_pa[6].buf(r_pa[6].f32(16 * i + 8, 16 * i + 16), "st4b%d" % i)) for i in range(2)])
        cctx = {}

        def c_s0(t):
            pss = [psring.next(), psring.next()]
            for hh in range(4):
                ps = pss[hh // 2]
                off = (hh % 2) * 256
                for j in range(2):
                    P.op('pe', lambda e, a=ps.ap[:, off:off + 256], l=qm_ap[:, 2 * hh + j, t * 128:(t + 1) * 128],
                         r=kTp.ap[:, 2 * hh + j, :], s=(j == 0), p=(j == 1): e.matmul(a, l, r, start=s, stop=p),
                         reads=[qmT, kTp], writes=[ps])
            sa_, sb__ = st4.next()
            for pr in range(2):
                P.op('dve', lambda e, o=sa_.ap[:, 2 * pr:2 * pr + 2], i=pss[pr].ap.rearrange("p (h m) -> p h m", h=2):
                     e.reduce_max(o, i, AX.X), reads=[pss[pr]], writes=[sa_])
            ts('dve', sa_.ap[:, 4:8], sa_.ap[:, 0:4], -SC, None, ALU.mult, None, [sa_], [sa_])
            P.op('dve', lambda e: e.memset(sb__.ap[:, 0:4], 0.0), reads=[], writes=[sb__])
            pe_ = pe4.next()
            for hh in range(4):
                ps = pss[hh // 2]
                off = (hh % 2) * 256
                act(pe_.ap[:, hh, :], ps.ap[:, off:off + 256], AF.Exp, [ps, sa_, sb__], [pe_, sb__],
                    bias=sa_.ap[:, 4 + hh:5 + hh], scale=SC, accum=sb__.ap[:, hh:hh + 1])
            P.op('dve', lambda e: e.reciprocal(sb__.ap[:, 4:8], sb__.ap[:, 0:4]), reads=[sb__], writes=[sb__])
            pb = pb4.next()
            tt('dve', pb.ap, pe_.ap, sb__.ap[:, 4:8].unsqueeze(2).to_broadcast([128, 4, 256]), ALU.mult, [pe_, sb__], [pb])
            cctx[t] = pb

        def c_s1(t):
            pb = cctx[t]
            pT = pT4.next()
            for pr in range(2):
                ps = psring.next()
                for q_ in range(4):
                    hh = 2 * pr + q_ // 2
                    mt = q_ % 2
                    trn(ps, ps.ap[:, q_ * 128:(q_ + 1) * 128], pb, pb.ap[:, hh, mt * 128:(mt + 1) * 128])
                evac(pT.ap[:, 2 * pr:2 * pr + 2, :, :], ps.ap.rearrange("p (h t n) -> p h t n", h=2, t=2), [ps], [pT])
            cctx[t] = pT

        def c_s2(t):
            pT = cctx.pop(t)
            for pr in range(2):
                ps_c = psring.next()
                for q_ in range(4):
                    ec = 4 * pr + q_
                    hh = ec // 2
                    for mt in range(2):
                        mm(ps_c, ps_c.ap[:, q_ * 128:(q_ + 1) * 128], vp, vp.ap[:, mt, ec * 128:(ec + 1) * 128],
                           pT, pT.ap[:, hh, mt, :], mt == 0, mt == 1)
                evac(cT[t].ap[:, 4 * pr:4 * pr + 4, :], ps_c.ap.rearrange("p (k n) -> p k n", k=4), [ps_c], [cT[t]])

        run_pipeline(list(range(NOWN)), [(0, c_s0), (1, c_s1), (2, c_s2)])
        for r_ in r_pa:
            A.drop(r_)
        r_sw = A.take(4 * 8 + 4 * 256 + 4 * 128 + 4 * 128, "swork")
        sst = r_sw.buf(r_sw.f32(0, 32).rearrange("p (h s) -> p h s", h=4), "sst")
        pex = Ring([r_sw.buf(r_sw.f32(32 + i * 256, 32 + (i + 1) * 256), "pex%d" % i) for i in range(4)])
        pbf = Ring([r_sw.buf(r_sw.bf(1056 + i * 128, 1056 + (i + 1) * 128), "pbf%d" % i) for i in range(4)])
        pTr = [r_sw.buf(r_sw.bf(1568 + i * 128, 1568 + (i + 1) * 128).rearrange("p (t n) -> p t n", t=2), "pT%d" % i)
               for i in range(4)]
        if _stop == "C3":
            P.emit(block, semctx)
            return nc
        A.drop(r_ckv)
        NPRESL = 10
        r_mss = [A.take(512, "msl%d" % i) for i in range(NPRESL)]
        slot_bufs = [r_mss[i].buf(r_mss[i].bf(0, 512).rearrange("p (k n) -> p k n", k=8), "msl%d" % i) for i in range(NPRESL)]

        def m_srcs(mc):
            return [(w_pa, mc, 0), (w_pb, mc, 0), (w_pb, mc, 8), (w_pc, mc, 0),
                    (w_in, 0 * D + mc, 0), (w_in, 0 * D + mc, 8), (w_in, 1 * D + mc, 0), (w_in, 1 * D + mc, 8),
                    (w_in, 2 * D + mc, 0), (w_in, 2 * D + mc, 8)]

        for i_, (wsrc, c0, k0) in enumerate(m_srcs(0)):
            load('pool', slot_bufs[i_], slot_bufs[i_].ap, wcols(wsrc, c0, 128, k0, 8))
        so = NOWN * 128
        ps_acc = [PS[i] for i in range(4)]
        tring = Ring([PS[i] for i in range(4, 8)])
        for b in range(16):
            if b < KPRE:
                ki = kpre[b]
            else:
                ki = kin.next()
                load('pool', ki, ki.ap, ck[b].rearrange("(t p) f -> p t f", p=128))
            kt = kTb.next()
            for mt in range(2):
                for q_ in range(2):
                    ps = tring.next()
                    pv = psb(ps)
                    for c4 in range(4):
                        c = q_ * 4 + c4
                        trn(ps, pv[:, c4 * 128:(c4 + 1) * 128], ki, ki.ap[:, mt, c * 128:(c + 1) * 128])
                    evac(kt.ap[:, q_ * 4:(q_ + 1) * 4, mt * 128:(mt + 1) * 128], pv.rearrange("p (k n) -> p k n", k=4), [ps], [kt])
            qb_ = qmb.next()
            tt('dve', qb_.ap, qm_ap[:, :, so:so + 128], blockmask[:, b:b + 1, :].to_broadcast([128, 8, 128]), ALU.mult,
               [qmT, CB], [qb_])
            for hh in range(4):
                for j in range(2):
                    mm(ps_acc[hh], ps_acc[hh].ap[:, 0:256], qb_, qb_.ap[:, 2 * hh + j, :], kt, kt.ap[:, 2 * hh + j, :],
                       b == 0 and j == 0, b == 15 and j == 1)
        if _stop == "C4":
            P.emit(block, semctx)
            return nc
        psring_saved = psring
        psring = tring
        for hh in range(4):
            softmax_to_pT(ps_acc[hh], hh, pTr[hh])
        if _stop == "C5":
            P.emit(block, semctx)
            return nc
        vring2 = Ring(list(vin.items) + list(kin.items))
        vring2.i = VPRE
        acc_c = [PS[0], PS[1]]
        for b in range(16):
            if b < VPRE:
                vi = vpre[b]
            else:
                vi = vring2.next()
                load('pool', vi, vi.ap, cv[b].rearrange("(t p) f -> p t f", p=128))
            for ec in range(8):
                pc = acc_c[ec // 4]
                for mt in range(2):
                    mm(pc, pc.ap[:, (ec % 4) * 128 + 8 * b:(ec % 4) * 128 + 8 * b + 8], vi, vi.ap[:, mt, ec * 128:(ec + 1) * 128],
                       pTr[ec // 2], pTr[ec // 2].ap[:, mt, 8 * b:8 * b + 8], mt == 0, mt == 1)
        for i in range(2):
            evac(cT[NOWN].ap[:, i * 4:(i + 1) * 4, :], acc_c[i].ap.rearrange("p (k n) -> p k n", k=4), [acc_c[i]], [cT[NOWN]])
        psring = psring_saved
        A.drop(r_sk)
        A.drop(r_sw)
        A.drop(r_qm)

        if _stop == "DBG":
            r_d = A.take(4096, "dbg")
            d1 = r_d.buf(r_d.f32(0, 2048), "d1")
            d2 = r_d.buf(r_d.f32(2048, 4096), "d2")
            so_ = NOWN * 128
            P.op('dve', lambda e: e.tensor_copy(d1.ap.rearrange("p (k n) -> p k n", k=16), bT_ap[:, :, so_:so_ + 128]), reads=[bT[NOWN]], writes=[d1])
            store('sp', y_smp[:, :], d1, d1.ap)
            P.op('dve', lambda e: e.tensor_copy(d2.ap[:, 0:1024].rearrange("p (k n) -> p k n", k=8), aT_ap[:, :, so_:so_ + 128]), reads=[aT], writes=[d2])
            P.op('dve', lambda e: e.tensor_copy(d2.ap[:, 1024:2048].rearrange("p (k n) -> p k n", k=8), cT_ap[:, :, so_:so_ + 128]), reads=[cT[NOWN]], writes=[d2])
            store('sp', y_own[0:128, :], d2, d2.ap)
            P.emit(block, semctx)
            return nc
        if _stop == "C":
            P.emit(block, semctx)
            return nc
        r_wo0 = A.take(4096, "wo0")
        WO0 = r_wo0.buf(r_wo0.bf(0, 4096).rearrange("p (k n) -> p k n", k=16), "wo0")
        r_mT = A.take(16 * NTOK // 2, "mT")
        mT_ap = r_mT.bf(0, 16 * NTOK // 2).rearrange("p (k n) -> p k n", k=16)
        mT = [r_mT.buf(mT_ap[:, :, t * 128:(t + 1) * 128], "mT%d" % t) for t in range(NT)]
        NSLOT = 14
        for i in range(NPRESL, NSLOT):
            r_mss.append(A.take(512, "msl%d" % i))
            slot_bufs.append(r_mss[i].buf(r_mss[i].bf(0, 512).rearrange("p (k n) -> p k n", k=8), "msl%d" % i))
        slots = Ring(slot_bufs)
        slots.i = NPRESL
        r_mws = [A.take(512, "mw%d" % i) for i in range(5)]
        sgm = [r_mws[i].buf(r_mws[i].f32(0, 512), "sgm%d" % i) for i in range(3)]
        acc_m = [r_mws[3 + i].buf(r_mws[3 + i].f32(0, 512), "accm%d" % i) for i in range(2)]
        for m in range(16):
            mc = m * 128
            if m == 11:
                load('pool', WO0, WO0.ap, wcols(w_out, 0, 512))
            sl = []
            if m == 0:
                sl = list(slot_bufs[0:NPRESL])
            else:
                for (wsrc, c0, k0) in m_srcs(mc):
                    s_ = slots.next()
                    load('pool', s_, s_.ap, wcols(wsrc, c0, 128, k0, 8))
                    sl.append(s_)
            for (o, n) in TG:
                def group(ps, wl, src_ap, src_bufs):
                    nk = len(wl) * 8
                    for k in range(nk):
                        w_ = wl[k // 8]
                        P.op('pe', lambda e, a=ps.ap[:, 0:n], l=w_.ap[:, k % 8, :], r=src_ap[:, k, o:o + n], s=(k == 0), p=(k == nk - 1):
                             e.matmul(a, l, r, start=s, stop=p), reads=[w_] + src_bufs, writes=[ps])
                ps_a, ps_b, ps_c = psring.next(), psring.next(), psring.next()
                ps_g = [psring.next(), psring.next(), psring.next()]
                group(ps_g[0], sl[4:6], xT_ap, xT_rd(o, n))
                group(ps_a, sl[0:1], aT_ap, [aT])
                group(ps_g[1], sl[6:8], xT_ap, xT_rd(o, n))
                group(ps_b, sl[1:3], bT_ap, tl(bT, o, n))
                group(ps_g[2], sl[8:10], xT_ap, xT_rd(o, n))
                group(ps_c, sl[3:4], cT_ap, tl(cT, o, n))
                for br in range(3):
                    act(sgm[br].ap[:, 0:n], ps_g[br].ap[:, 0:n], AF.Sigmoid, [ps_g[br]], [sgm[br]])
                tt('dve', acc_m[0].ap[:, 0:n], sgm[0].ap[:, 0:n], ps_a.ap[:, 0:n], ALU.mult, [sgm[0], ps_a], [acc_m[0]])
                tt('dve', acc_m[1].ap[:, 0:n], sgm[1].ap[:, 0:n], ps_b.ap[:, 0:n], ALU.mult, [sgm[1], ps_b], [acc_m[1]])
                tt('dve', acc_m[0].ap[:, 0:n], acc_m[0].ap[:, 0:n], acc_m[1].ap[:, 0:n], ALU.add, [acc_m[0], acc_m[1]], [acc_m[0]])
                tt('dve', acc_m[1].ap[:, 0:n], sgm[2].ap[:, 0:n], ps_c.ap[:, 0:n], ALU.mult, [sgm[2], ps_c], [acc_m[1]])
                tt('dve', mT_ap[:, m, o:o + n], acc_m[0].ap[:, 0:n], acc_m[1].ap[:, 0:n], ALU.add, [acc_m[0], acc_m[1]], tl(mT, o, n))
        for r_ in r_mws + r_mss:
            A.drop(r_)
        A.drop(r_xT)
        A.drop(r_aT)
        A.drop(r_bT)
        A.drop(r_cT)

        if _stop == "M":
            P.emit(block, semctx)
            return nc
        r_r = A.take(NT * 2048, "resid")
        R = [r_r.buf(r_r.f32(t * 2048, (t + 1) * 2048), "r%d" % t) for t in range(NT)]
        r_ln = A.take(2 * 2048, "lntab")
        LNT = r_ln.buf(r_ln.f32(0, 4096), "lnt")
        load('sp', LNT, LNT.ap, vec_ln[0:1, 0:4096].partition_broadcast(128))
        r_x1b = A.take(2 * 1024, "x1b")
        x1b = Ring([r_x1b.buf(r_x1b.bf(i * 1024, (i + 1) * 1024), "x1b%d" % i) for i in range(2)])
        r_wos = [r_wo0] + [A.take(4096, "wo%d" % i) for i in range(1, 4)]
        WO = [WO0] + [r_wos[i].buf(r_wos[i].bf(0, 4096).rearrange("p (k n) -> p k n", k=16), "wo%d" % i) for i in range(1, 4)]
        for n in range(1, 4):
            load('pool', WO[n], WO[n].ap, wcols(w_out, n * 512, 512))
        r_xrs = [A.take(512, "xres%d" % i) for i in range(3)]
        xrr = Ring([r_xrs[i].buf(r_xrs[i].f32(0, 512), "xres%d" % i) for i in range(3)])
        r_lw = A.take(64, "lnwork")
        lstr = Ring([r_lw.buf(r_lw.f32(i * 32, i * 32 + 32), "lst%d" % i) for i in range(2)])

        def layer_norm_a(rt):
            ls = lstr.next()
            st6 = ls.ap[:, 0:24].rearrange("p (c s) -> p c s", c=4)
            for c in range(4):
                P.op('dve', lambda e, a=st6[:, c, :], b=rt.ap[:, c * 512:(c + 1) * 512]: e.bn_stats(a, b), reads=[rt], writes=[ls])
            P.op('dve', lambda e: e.bn_aggr(ls.ap[:, 24:26], ls.ap[:, 0:24]), reads=[ls], writes=[ls])
            act(ls.ap[:, 27:28], ls.ap[:, 25:26], AF.Sqrt, [ls, EPSB], [ls], bias=EPSB.ap[:, 0:1])
            P.op('dve', lambda e: e.reciprocal(ls.ap[:, 28:29], ls.ap[:, 27:28]), reads=[ls], writes=[ls])
            stt(ls.ap[:, 29:30], ls.ap[:, 24:25], -1.0, ls.ap[:, 28:29], ALU.mult, ALU.mult, [ls], [ls])
            act(rt.ap, rt.ap, AF.Identity, [rt, ls], [rt], bias=ls.ap[:, 29:30], scale=ls.ap[:, 28:29])

        def layer_norm_b(rt, out_bf=None, final_scale=None):
            tt('dve', rt.ap, rt.ap, LNT.ap[:, 0:2048], ALU.mult, [rt, LNT], [rt])
            tt('pool', rt.ap, rt.ap, LNT.ap[:, 2048:4096], ALU.add, [rt, LNT], [rt])
            if out_bf is not None:
                evac(out_bf.ap, rt.ap, [rt], [out_bf], eng='act')
                act(rt.ap, rt.ap, AF.Copy, [rt], [rt], scale=final_scale)

        x1T = mT
        octx = {}

        def o_s0(t):
            for n in range(4):
                xr = xrr.next()
                load('sp', xr, xr.ap, xrows(t)[:, n * 512:(n + 1) * 512])
                ps = psring.next()
                for k in range(16):
                    mm(ps, ps.ap, mT[t], mT[t].ap[:, k, :], WO[n], WO[n].ap[:, k, :], k == 0, k == 15)
                stt(R[t].ap[:, n * 512:(n + 1) * 512], xr.ap, ALPHA, ps.ap, ALU.mult, ALU.add, [xr, ps], [R[t]])
            layer_norm_a(R[t])

        def o_s05(t):
            xb_ = x1b.next()
            layer_norm_b(R[t], out_bf=xb_, final_scale=ALPHA)
            octx[t] = xb_

        def o_s1(t):
            xb_ = octx.pop(t)
            for hf in range(4):
                ps = psring.next()
                pv = psb(ps)
                for k in range(4):
                    kk = hf * 4 + k
                    trn(ps, pv[:, k * 128:(k + 1) * 128], xb_, xb_.ap[:, kk * 128:(kk + 1) * 128])
                evac(x1T[t].ap[:, hf * 4:(hf + 1) * 4, :], pv.rearrange("p (k n) -> p k n", k=4), [ps], [x1T[t]])

        run_pipeline(list(range(NT)), [(0, o_s0), (1, o_s05), (2, o_s1)])
        for r_ in r_wos:
            A.drop(r_)
        for r_ in r_xrs:
            A.drop(r_)
        A.drop(r_x1b)
        if _stop == "O":
            P.emit(block, semctx)
            return nc

        GC = 4
        NG = 44 // GC
        r_hT = A.take(2 * GC * NTOK // 2, "hT")
        hTr = Ring([r_hT.buf(r_hT.bf(i * GC * NTOK // 2, (i + 1) * GC * NTOK // 2).rearrange("p (k n) -> p k n", k=GC), "hT%d" % i)
                    for i in range(2)])
        r_wgs = [A.take(1024, "wgu%d" % i) for i in range(4)]
        wgr = Ring([r_wgs[i].buf(r_wgs[i].bf(0, 1024).rearrange("p (k n) -> p k n", k=16), "wgu%d" % i) for i in range(4)])
        r_wds = [A.take(GC * 256, "wd%d" % i) for i in range(8)]
        wdr = Ring([r_wds[i].buf(r_wds[i].bf(0, GC * 256).rearrange("p (k n) -> p k n", k=GC), "wd%d" % i) for i in range(8)])
        r_sg = A.take(3 * 512, "sgt")
        sgr = Ring([r_sg.buf(r_sg.f32(i * 512, (i + 1) * 512), "sgt%d" % i) for i in range(3)])
        for g in range(NG):
            hT = hTr.next()
            for cl in range(GC):
                c = g * GC + cl
                wg = wgr.next()
                wu_ = wgr.next()
                load('pool', wg, wg.ap, wcols(w_fg, c * 128, 128))
                load('pool', wu_, wu_.ap, wcols(w_fu, c * 128, 128))
                for (o, n) in TG:
                    ps_g, ps_u = psring.next(), psring.next()
                    for (ps, w_) in ((ps_g, wg), (ps_u, wu_)):
                        for k in range(16):
                            P.op('pe', lambda e, a=ps.ap[:, 0:n], l=w_.ap[:, k, :], r=mT_ap[:, k, o:o + n], s=(k == 0), p=(k == 15):
                                 e.matmul(a, l, r, start=s, stop=p), reads=[w_] + tl(x1T, o, n), writes=[ps])
                    sg = sgr.next()
                    act(sg.ap[:, 0:n], ps_g.ap[:, 0:n], AF.Silu, [ps_g], [sg])
                    tt('dve', hT.ap[:, cl, o:o + n], sg.ap[:, 0:n], ps_u.ap[:, 0:n], ALU.mult, [sg, ps_u], [hT])
            wds = []
            for n in range(4):
                wd = wdr.next()
                load('pool', wd, wd.ap, wcols(w_fd, n * 512, 512, g * GC, GC))
                wds.append(wd)
            if g == NG - 1:
                load('sp', LNT, LNT.ap, vec_ln[0:1, 4096:8192].partition_broadcast(128))
            if g == NG - 2:
                prev_tail = (hT, wds)
            elif g < NG - 2:
                for n in range(4):
                    for t in range(NT):
                        ps = psring.next()
                        for k in range(GC):
                            mm(ps, ps.ap, hT, hT.ap[:, k, t * 128:(t + 1) * 128], wds[n], wds[n].ap[:, k, :], k == 0, k == GC - 1)
                        tt('dve', R[t].ap[:, n * 512:(n + 1) * 512], R[t].ap[:, n * 512:(n + 1) * 512], ps.ap, ALU.add, [R[t], ps], [R[t]])
            else:
                def f_s0(t, hT=hT, wds=wds, prev_tail=prev_tail):
                    hTp, wdp = prev_tail
                    for n in range(4):
                        ps = psring.next()
                        for k in range(GC):
                            mm(ps, ps.ap, hTp, hTp.ap[:, k, t * 128:(t + 1) * 128], wdp[n], wdp[n].ap[:, k, :], k == 0, False)
                        for k in range(GC):
                            mm(ps, ps.ap, hT, hT.ap[:, k, t * 128:(t + 1) * 128], wds[n], wds[n].ap[:, k, :], False, k == GC - 1)
                        tt('dve', R[t].ap[:, n * 512:(n + 1) * 512], R[t].ap[:, n * 512:(n + 1) * 512], ps.ap, ALU.add, [R[t], ps], [R[t]])
                    layer_norm_a(R[t])

                def f_s1(t):
                    layer_norm_b(R[t])
                    store('sp', yrows(t), R[t], R[t].ap)

                run_pipeline(list(range(NT)), [(0, f_s0), (1, f_s1)])
        P.emit(block, semctx)
    return nc


def _core_consts(half):
    f32 = np.float32
    hd = 64
    inv = 10000.0 ** (-(np.arange(hd, dtype=np.float64) / float(hd)))
    p = np.arange(128)
    cos2 = np.zeros((128, 17, 128), f32)
    sin2 = np.zeros((128, 17, 128), f32)
    for ti in range(17):
        if ti < 8:
            pos = (ti * 128 + p).astype(np.float64)
        elif ti < 16:
            pos = (half * 1024 + (ti - 8) * 128 + p).astype(np.float64)
        else:
            pos = (16384 + (p % 8)).astype(np.float64)
        ang = pos[:, None] * inv[None, :]
        c = np.cos(ang).astype(f32)
        s = np.sin(ang).astype(f32)
        cos2[:, ti, :64] = c
        cos2[:, ti, 64:] = c
        sin2[:, ti, :64] = -s
        sin2[:, ti, 64:] = s
    hh = np.arange(8, dtype=np.float64)
    lg = np.log1p(-(2.0 ** (-5.0 - hh)))
    qs = np.zeros((128, 16), f32)
    ks = np.zeros((128, 16), f32)
    ip = p.astype(np.float64)
    isq = (p % 8).astype(np.float64)
    qs[:, 0:8] = np.exp((ip[:, None] + 1.0) * lg[None, :])
    ks[:, 0:8] = np.exp(-(ip[:, None] + 1.0) * lg[None, :]) * (128.0 ** -0.5)
    qs[:, 8:16] = np.exp((isq[:, None] + 1.0) * lg[None, :])
    ks[:, 8:16] = np.exp(-(isq[:, None] + 1.0) * lg[None, :]) * (128.0 ** -0.5)
    rowmask = (p[:, None] // 8 == np.arange(16)[None, :]).astype(f32)
    cst_f = np.concatenate([cos2.reshape(128, -1), sin2.reshape(128, -1), qs, ks, rowmask], axis=1).astype(f32)
    ident = np.eye(128, dtype=f32)
    j = p[:, None]
    i = p[None, :]
    maskT_p = (i >= j).astype(f32)
    maskT_s = ((i >= j) & (i // 8 == j // 8)).astype(f32)
    bm = (np.arange(128)[None, :] // 8 == np.arange(16)[:, None]).astype(f32)
    blockmask = np.broadcast_to(bm.reshape(1, 16 * 128), (128, 16 * 128))
    cst_b = np.concatenate([ident, maskT_p, maskT_s, blockmask], axis=1).astype(f32)
    return np.ascontiguousarray(cst_f), np.ascontiguousarray(cst_b)


_NC_CACHE = {}


def kernel(x_prompt, x_sample, mem_prompt, state_ret, cache_mem_k, cache_mem_v,
           w_in, sgu_ln_g, sgu_ln_b, sgu_w, sgu_b, w_proj_a, ret_gn_g, w_proj_b,
           w_mem_k, w_mem_v, w_proj_c, w_out, ln1_g, ln1_b,
           w_ffn_gate, w_ffn_up, w_ffn_down, ln2_g, ln2_b):
    f32 = np.float32
    A_ = lambda a: np.ascontiguousarray(np.asarray(a, dtype=f32))
    x_prompt, x_sample, mem_prompt = A_(x_prompt), A_(x_sample), A_(mem_prompt)
    state_ret, cache_mem_k, cache_mem_v = A_(state_ret), A_(cache_mem_k), A_(cache_mem_v)
    shared = {
        "w_in": A_(w_in)[0], "w_pa": A_(w_proj_a)[0], "w_pb": A_(w_proj_b)[0], "w_pc": A_(w_proj_c)[0],
        "w_mk": A_(w_mem_k)[0], "w_mv": A_(w_mem_v)[0], "w_out": A_(w_out)[0],
        "w_fg": A_(w_ffn_gate)[0], "w_fu": A_(w_ffn_up)[0], "w_fd": A_(w_ffn_down)[0],
    }
    sw = A_(sgu_w)[0]
    wT_p = sw.transpose(2, 0, 1)
    wT_s = np.zeros((128, 4, 128), f32)
    for b in range(16):
        wT_s[8 * b:8 * b + 8, :, 8 * b:8 * b + 8] = sw[:, :8, :8].transpose(2, 0, 1)
    shared["sgu_wT"] = np.ascontiguousarray(np.stack([wT_p, wT_s], axis=1))
    sb = A_(sgu_b)[0]
    sb_s = np.tile(sb[:, :8], (1, 16))
    shared["sgu_bb"] = np.ascontiguousarray(np.stack([sb, sb_s], axis=0).reshape(1, -1))
    shared["vec_a"] = np.ascontiguousarray(np.concatenate([A_(sgu_ln_g)[0].reshape(-1), A_(sgu_ln_b)[0].reshape(-1)]).reshape(1, -1))
    shared["vec_g"] = np.ascontiguousarray(A_(ret_gn_g)[0].reshape(1, -1))
    shared["vec_ln"] = np.ascontiguousarray(np.concatenate([A_(ln1_g)[0], A_(ln1_b)[0], A_(ln2_g)[0], A_(ln2_b)[0]]).reshape(1, -1))

    in_maps = []
    for c in range(8):
        b, half = c // 2, c % 2
        cf, cb = _core_consts(half)
        m = dict(shared)
        m["x_own"] = np.ascontiguousarray(x_prompt[b, half * 1024:(half + 1) * 1024])
        m["x_pre"] = np.ascontiguousarray(x_prompt[b, 0:1024]) if half == 1 else np.zeros((1024, D), f32)
        m["x_smp"] = np.ascontiguousarray(x_sample[16 * c:16 * c + 16].reshape(128, D))
        m["mem"] = np.ascontiguousarray(mem_prompt[b])
        m["state"] = np.ascontiguousarray(state_ret[0, 16 * c:16 * c + 16])
        m["ck"] = np.ascontiguousarray(cache_mem_k[0, 16 * c:16 * c + 16].reshape(16, 256, 1024))
        m["cv"] = np.ascontiguousarray(cache_mem_v[0, 16 * c:16 * c + 16].reshape(16, 256, 1024))
        m["cst_f"] = cf
        m["cst_b"] = cb
        in_maps.append(m)

    if _os.environ.get("MK_ONECORE"):
        return in_maps
    if "nc" not in _NC_CACHE:
        _NC_CACHE["nc"] = build_program()
    nc = _NC_CACHE["nc"]
    res = run_bass_kernel_spmd(nc, in_maps, core_ids=list(range(8)))
    R = res.results
    y_prompt = np.zeros((4, 2048, D), f32)
    y_sample = np.zeros((128, 8, D), f32)
    st_p = np.zeros((1, 4, 8, 128, 256), f32)
    mk = np.zeros((1, 4, 256, 4, 256), f32)
    mv = np.zeros((1, 4, 256, 4, 256), f32)
    st_s = np.zeros((1, 128, 8, 128, 256), f32)
    cvs = np.zeros((1, 128, 8, 4, 256), f32)
    for c in range(8):
        b, half = c // 2, c % 2
        r = R[c]
        y_prompt[b, half * 1024:(half + 1) * 1024] = r["y_own"]
        y_sample[16 * c:16 * c + 16] = r["y_smp"].reshape(16, 8, D)
        st_s[0, 16 * c:16 * c + 16] = r["st_s"]
        cvs[0, 16 * c:16 * c + 16] = r["cv_s"].reshape(16, 8, 4, 256)
        if half == 1:
            st_p[0, b] = r["st_p"]
        else:
            mk[0, b] = r["mk_o"].reshape(256, 4, 256)
            mv[0, b] = r["mv_o"].reshape(256, 4, 256)
    return (y_prompt, y_sample, st_p, mk, mv, st_s, cvs)
```
